# Optimizing a Trainium2 kernel written in Bass

```python
import jax, jax.numpy as jnp
from jax import lax
import numpy as np

D_MODEL = 1024
BATCH = 8
SEQ = 2048
DEPTH = 2
DEC_BATCH = 128
DEC_SEQ = 4
PAST_LEN = 16384
PAGE_SIZE = 128

CONV_WIDTH = 3
W_CONV = D_MODEL
D_SG = D_MODEL
SG_CHUNK = 128
SG_GROUPS = 4
N_MEM = 256
MEM_HEADS = 4
MEM_HEAD_DIM = D_MODEL // MEM_HEADS
D_ATT = MEM_HEADS * MEM_HEAD_DIM
N_BRANCH = 3
D_FF = 2816
D_IN = 3 * W_CONV + 2 * D_SG + D_ATT + N_BRANCH * D_MODEL
EPS = 1e-6

kernel_name = "gated_branch_shortconv_sgu_memattn_convffn_step"


def _rmsnorm(x, g):
    xf = x.astype(jnp.float32)
    y = xf * lax.rsqrt(jnp.mean(xf * xf, axis=-1, keepdims=True) + EPS)
    return (y * g.astype(jnp.float32)).astype(x.dtype)


def _layernorm(x, g):
    xf = x.astype(jnp.float32)
    mu = jnp.mean(xf, axis=-1, keepdims=True)
    xc = xf - mu
    y = xc * lax.rsqrt(jnp.mean(xc * xc, axis=-1, keepdims=True) + EPS)
    return (y * g.astype(jnp.float32)).astype(x.dtype)


def _causal_dwconv3(x, prev, w):
    xp = jnp.concatenate([prev.astype(x.dtype), x], axis=1)
    y = w[0] * xp[:, :-2] + w[1] * xp[:, 1:-1] + w[2] * xp[:, 2:]
    return y, xp[:, -(CONV_WIDTH - 1):]


def _spatial_gate(v, sg_w, sg_b):
    bsz, t, _ = v.shape
    L = min(t, SG_CHUNK)
    n_chunks = t // L
    mask = jnp.tril(jnp.ones((L, L), dtype=bool))
    w = jnp.where(mask[None], sg_w[:, :L, :L], jnp.zeros((), sg_w.dtype))
    vr = v.reshape(bsz, n_chunks, L, SG_GROUPS, D_SG // SG_GROUPS)
    s = jnp.einsum('gts,bcsgd->bctgd', w, vr) + sg_b[:, :L].T[None, None, :, :, None]
    return s.reshape(bsz, t, D_SG)


def _mem_attention(q, mem_k, mem_v):
    bsz, t, _ = q.shape
    qh = q.reshape(bsz, t, MEM_HEADS, MEM_HEAD_DIM)
    s = jnp.einsum('bthd,bmhd->bhtm', qh.astype(jnp.float32), mem_k.astype(jnp.float32))
    p = jax.nn.softmax(s * (MEM_HEAD_DIM ** -0.5), axis=-1).astype(q.dtype)
    o = jnp.einsum('bhtm,bmhd->bthd', p, mem_v.astype(q.dtype))
    return o.reshape(bsz, t, D_ATT)


def _mem_kv(mem, norm_mem_g, w_k, w_v):
    m = _rmsnorm(mem, norm_mem_g)
    bsz = mem.shape[0]
    k = (m @ w_k).reshape(bsz, N_MEM, MEM_HEADS, MEM_HEAD_DIM)
    v = (m @ w_v).reshape(bsz, N_MEM, MEM_HEADS, MEM_HEAD_DIM)
    return k, v


def _layer(x, conv_a_prev, conv_f_prev, mem_k, mem_v, norm_mix_g, w_in, conv_a_w, sg_ln_g,
           sg_w, sg_b, w_o, norm_ffn_g, w_up, conv_f_w, conv_f_b, w_down):
    z = _rmsnorm(x, norm_mix_g)
    p = z @ w_in
    cuts = np.cumsum([W_CONV, W_CONV, W_CONV, D_SG, D_SG, D_ATT]).tolist()
    a_h, a_c, a_b, u, v, q, g = jnp.split(p, cuts, axis=-1)
    conv_out, conv_a_state = _causal_dwconv3(a_c * a_h, conv_a_prev, conv_a_w)
    y_a = a_b * conv_out
    v_n = _layernorm(v, sg_ln_g)
    y_b = u * _spatial_gate(v_n, sg_w, sg_b)
    y_m = _mem_attention(q, mem_k, mem_v)
    g = jax.nn.sigmoid(g.astype(jnp.float32)).astype(x.dtype)
    g_a, g_b, g_m = jnp.split(g, N_BRANCH, axis=-1)
    x = x + (g_a * y_a + g_b * y_b + g_m * y_m) @ w_o
    h = _rmsnorm(x, norm_ffn_g) @ w_up
    hc, conv_f_state = _causal_dwconv3(h, conv_f_prev, conv_f_w)
    hc = hc + conv_f_b
    a, gt = jnp.split(hc, 2, axis=-1)
    x = x + (jax.nn.silu(gt) * a) @ w_down
    return x, conv_a_state, conv_f_state, v_n


def setup_inputs(seed: int = 0) -> dict:
    key = jax.random.key(seed)
    ks = jax.random.split(key, 24)
    f32 = jnp.float32

    def nrm(k, shape, scale=1.0):
        return jax.random.normal(k, shape, f32) * scale

    def gain(k, shape):
        return 1.0 + 0.01 * jax.random.normal(k, shape, f32)

    return {
        "x_prompt": nrm(ks[0], (BATCH, SEQ, D_MODEL)),
        "x_sample": nrm(ks[1], (DEC_BATCH, DEC_SEQ, D_MODEL)),
        "mem_prompt": nrm(ks[2], (BATCH, N_MEM, D_MODEL)),
        "cache_conv_a": nrm(ks[3], (DEPTH, DEC_BATCH, CONV_WIDTH - 1, W_CONV)),
        "cache_conv_ffn": nrm(ks[4], (DEPTH, DEC_BATCH, CONV_WIDTH - 1, 2 * D_FF)),
        "cache_mem_k": nrm(ks[5], (DEPTH, DEC_BATCH, N_MEM, MEM_HEADS, MEM_HEAD_DIM)),
        "cache_mem_v": nrm(ks[6], (DEPTH, DEC_BATCH, N_MEM, MEM_HEADS, MEM_HEAD_DIM)),
        "norm_mix_g": gain(ks[7], (DEPTH, D_MODEL)),
        "w_in": nrm(ks[8], (DEPTH, D_MODEL, D_IN), D_MODEL ** -0.5),
        "conv_a_w": nrm(ks[9], (DEPTH, CONV_WIDTH, W_CONV), CONV_WIDTH ** -0.5),
        "sg_ln_g": gain(ks[10], (DEPTH, D_SG)),
        "sg_w": nrm(ks[11], (DEPTH, SG_GROUPS, SG_CHUNK, SG_CHUNK), SG_CHUNK ** -0.5),
        "sg_b": gain(ks[12], (DEPTH, SG_GROUPS, SG_CHUNK)),
        "norm_mem_g": gain(ks[13], (DEPTH, D_MODEL)),
        "w_k": nrm(ks[14], (DEPTH, D_MODEL, D_ATT), D_MODEL ** -0.5),
        "w_v": nrm(ks[15], (DEPTH, D_MODEL, D_ATT), D_MODEL ** -0.5),
        "w_o": nrm(ks[16], (DEPTH, D_MODEL, D_MODEL), D_MODEL ** -0.5),
        "norm_ffn_g": gain(ks[17], (DEPTH, D_MODEL)),
        "w_up": nrm(ks[18], (DEPTH, D_MODEL, 2 * D_FF), D_MODEL ** -0.5),
        "conv_f_w": nrm(ks[19], (DEPTH, CONV_WIDTH, 2 * D_FF), CONV_WIDTH ** -0.5),
        "conv_f_b": nrm(ks[20], (DEPTH, 2 * D_FF), 0.01),
        "w_down": nrm(ks[21], (DEPTH, D_FF, D_MODEL), D_FF ** -0.5),
        "norm_final_g": gain(ks[22], (D_MODEL,)),
    }


def reference(x_prompt, x_sample, mem_prompt, cache_conv_a, cache_conv_ffn, cache_mem_k,
              cache_mem_v, norm_mix_g, w_in, conv_a_w, sg_ln_g, sg_w, sg_b, norm_mem_g, w_k,
              w_v, w_o, norm_ffn_g, w_up, conv_f_w, conv_f_b, w_down, norm_final_g):
    xp, xs = x_prompt, x_sample
    bp = x_prompt.shape[0]
    pa_list, pf_list, pk_list, pv_list = [], [], [], []
    sa_list, sf_list, sv_list = [], [], []
    for l in range(DEPTH):
        shared = (norm_mix_g[l], w_in[l], conv_a_w[l], sg_ln_g[l], sg_w[l], sg_b[l], w_o[l],
                  norm_ffn_g[l], w_up[l], conv_f_w[l], conv_f_b[l], w_down[l])
        mk, mv = _mem_kv(mem_prompt, norm_mem_g[l], w_k[l], w_v[l])
        zero_a = jnp.zeros((bp, CONV_WIDTH - 1, W_CONV), xp.dtype)
        zero_f = jnp.zeros((bp, CONV_WIDTH - 1, 2 * D_FF), xp.dtype)
        xp, pa, pf, _ = _layer(xp, zero_a, zero_f, mk, mv, *shared)
        pa_list.append(pa)
        pf_list.append(pf)
        pk_list.append(mk)
        pv_list.append(mv)
        xs, sa, sf, sv = _layer(xs, cache_conv_a[l], cache_conv_ffn[l], cache_mem_k[l],
                                cache_mem_v[l], *shared)
        sa_list.append(sa)
        sf_list.append(sf)
        sv_list.append(sv)
    y_prompt = _rmsnorm(xp, norm_final_g)
    y_sample = _rmsnorm(xs, norm_final_g)
    return (y_prompt, y_sample,
            jnp.stack(pa_list), jnp.stack(pf_list), jnp.stack(pk_list), jnp.stack(pv_list),
            jnp.stack(sa_list), jnp.stack(sf_list), jnp.stack(sv_list))
```

```python
import itertools
import os
import numpy as np
import concourse.bass as bass
import concourse.mybir as mybir
from concourse.bass_utils import run_bass_kernel_spmd

F32 = mybir.dt.float32
BF16 = mybir.dt.bfloat16
AF = mybir.ActivationFunctionType
ALU = mybir.AluOpType
SZ = {F32: 4, BF16: 2}

D = 1024
DFF = 2816
DIN = 9216
NMEM = 256
EPS = 1e-6
NCORES = 8
SEQ = 2048
NSB = 16
NST = 64

OFF_H, OFF_C, OFF_B, OFF_U, OFF_V, OFF_Q, OFF_GA, OFF_GB, OFF_GM = (
    0, 1024, 2048, 3072, 4096, 5120, 6144, 7168, 8192)

R_GMIX, R_GFFN, R_GMEM, R_GSG, R_GFIN, R_CAW = 0, 2, 4, 6, 8, 9
NV1 = 15
NV2 = 8

GROUPS = [(0, 6, False), (768, 6, False), (1536, 4, True)]


def _intervals(ap):
    esz = SZ[ap.dtype]
    pairs = list(ap.ap)
    pstep = pairs[0][0]
    off = ap.offset % pstep if pstep else ap.offset
    free = [(s, c) for s, c in pairs[1:] if c > 1 and s != 0]
    if not free:
        if ap.tensor.name == "ps":
            return [((off * esz // 2048) * 2048, (off * esz // 2048) * 2048 + 2048)]
        return [(off * esz, (off + 1) * esz)]
    if free[-1][0] == 1:
        inner = free[-1][1]
        outer = free[:-1]
    else:
        inner = 1
        outer = free
    n_outer = 1
    for _, c in outer:
        n_outer *= c
    if n_outer > 48:
        hi = off + sum(s * (c - 1) for s, c in free) + 1
        if ap.tensor.name == "ps":
            return [((off * esz // 2048) * 2048, ((hi * esz + 2047) // 2048) * 2048)]
        return [(off * esz, hi * esz)]
    res = []
    for idx in itertools.product(*[range(c) for _, c in outer]):
        o = off + sum(i * s for i, (s, _) in zip(idx, outer))
        res.append((o * esz, (o + inner) * esz))
    if ap.tensor.name == "ps":
        res = [((lo // 2048) * 2048, ((hi + 2047) // 2048) * 2048) for lo, hi in res]
    res.sort()
    merged = [res[0]]
    for lo, hi in res[1:]:
        if lo <= merged[-1][1]:
            merged[-1] = (merged[-1][0], max(hi, merged[-1][1]))
        else:
            merged.append((lo, hi))
    return merged


def _onchip(ap):
    return "DRAM" not in str(ap.space).upper() and "HBM" not in str(ap.space).upper()


class _Op:
    __slots__ = ("eng", "fn", "deps", "is_dma", "signaled", "seq", "dsem", "dval", "idx", "dprev", "raw")


class Prog:
    ENGS = ("pe", "act", "dve", "pool", "sp")
    DMA_POOL = 24

    def __init__(self):
        self.ops = []
        self.streams = {e: [] for e in self.ENGS}
        self.recs = {}
        self.ndma = {e: 0 for e in self.ENGS}
        self.dma_hist = {e: [] for e in self.ENGS}
        self.enabled = True
        self.phase_i = 0
        self.phase_limit = int(os.environ.get("KSTOP", "1000000"))
        self.phase_log = []

    def phase(self, name):
        self.phase_log.append((self.phase_i, name, len(self.ops)))
        self.phase_i += 1
        if self.phase_i > self.phase_limit:
            self.enabled = False

    def add(self, eng, fn, reads=(), writes=(), dma=False):
        if not self.enabled:
            return None
        op = _Op()
        op.eng, op.fn, op.is_dma = eng, fn, dma
        op.signaled = False
        op.seq = 0
        op.idx = len(self.ops)
        op.dprev = None
        op.dsem = op.dval = 0
        deps = set()
        for ap in reads:
            if _onchip(ap):
                if ap.tensor.name == "ps":
                    self._write(ap, op, deps, is_read=True)
                else:
                    self._read(ap, op, deps)
        op.raw = set(deps)
        for ap in writes:
            if _onchip(ap):
                self._write(ap, op, deps)
        op.deps = deps
        if dma:
            hist = self.dma_hist[eng]
            i = len(hist)
            op.dsem = i % self.DMA_POOL
            op.dval = 16 * (i // self.DMA_POOL + 1)
            if i >= self.DMA_POOL:
                op.dprev = hist[i - self.DMA_POOL]
            hist.append(op.idx)
            self.ndma[eng] = i + 1
        self.ops.append(op)
        self.streams[eng].append(op)
        return op

    def _read(self, ap, op, deps):
        recs = self.recs.setdefault(ap.tensor.name, [])
        for lo, hi in _intervals(ap):
            hit = False
            for r in recs:
                if r[0] < hi and lo < r[1]:
                    hit = True
                    if r[2] is not None:
                        deps.add(r[2])
                    if op.is_dma:
                        r[4].append(op.idx)
                    else:
                        r[3][op.eng] = op.idx
            if not hit:
                recs.append([lo, hi, None, ({} if op.is_dma else {op.eng: op.idx}),
                             ([op.idx] if op.is_dma else [])])

    def _write(self, ap, op, deps, is_read=False):
        name = ap.tensor.name
        recs = self.recs.setdefault(name, [])
        for lo, hi in _intervals(ap):
            keep = []
            for r in recs:
                if r[0] < hi and lo < r[1]:
                    if r[2] is not None:
                        same_eng_rr = (is_read and len(r) > 5 and r[5] and self.ops[r[2]].eng == op.eng
                                       and not op.is_dma)
                        if not same_eng_rr:
                            deps.add(r[2])
                    deps.update(r[3].values())
                    deps.update(r[4])
                    kind = r[5] if len(r) > 5 else False
                    if r[0] < lo:
                        keep.append([r[0], lo, r[2], dict(r[3]), list(r[4]), kind])
                    if hi < r[1]:
                        keep.append([hi, r[1], r[2], dict(r[3]), list(r[4]), kind])
                else:
                    keep.append(r)
            keep.append([lo, hi, op.idx, {}, [], is_read])
            recs = keep
        self.recs[name] = recs

    def finalize(self):
        ops = self.ops
        for op in ops:
            need = set()
            for d in op.deps:
                if d == op.idx:
                    continue
                dop = ops[d]
                if dop.is_dma:
                    need.add(d)
                elif dop.eng != op.eng or op.eng != "pe":
                    dop.signaled = True
                    need.add(d)
            if op.dprev is not None:
                need.add(op.dprev)
            op.deps = need
        for e in self.ENGS:
            c = 0
            for op in self.streams[e]:
                if op.signaled and not op.is_dma:
                    c += 1
                    op.seq = c

    def emit(self, nc, block, csems, dsems):
        ops = self.ops
        hooks = {"pe": block.tensor, "act": block.scalar, "dve": block.vector,
                 "pool": block.gpsimd, "sp": block.sync}

        def make(ename):
            def body(e):
                waited = {}
                for op in self.streams[ename]:
                    want = {}
                    for d in op.deps:
                        dop = ops[d]
                        if dop.is_dma:
                            key = ("d", dop.eng, dop.dsem)
                            val = dop.dval
                        else:
                            key = ("c", dop.eng)
                            val = dop.seq
                        if val > want.get(key, 0):
                            want[key] = val
                    for key, val in want.items():
                        if waited.get(key, 0) >= val:
                            continue
                        sem = dsems[key[1]][key[2]] if key[0] == "d" else csems[key[1]]
                        e.wait_ge(sem, val)
                        waited[key] = val
                    inst = op.fn(e)
                    if op.is_dma:
                        inst.then_inc(dsems[ename][op.dsem], 16)
                    elif op.signaled:
                        inst.then_inc(csems[ename], 1)
                n = self.ndma[ename]
                for s in range(min(n, self.DMA_POOL)):
                    cnt = (n - 1 - s) // self.DMA_POOL + 1
                    e.wait_ge(dsems[ename][s], 16 * cnt)
            return body

        for ename in self.ENGS:
            if self.streams[ename]:
                hooks[ename](make(ename))


class Ring:
    def __init__(self, views):
        self.views = views
        self.i = 0

    def get(self):
        v = self.views[self.i % len(self.views)]
        self.i += 1
        return v


def build_program():
    nc = bass.Bass("TRN2", target_bir_lowering=False)
    P = Prog()

    def din(name, shape):
        return nc.dram_tensor(name, list(shape), F32, kind="ExternalInput").ap()

    def dout(name, shape):
        return nc.dram_tensor(name, list(shape), F32, kind="ExternalOutput").ap()

    xp = din("xp", [SEQ, D])
    xs = din("xs", [NST, D])
    mem = din("mem", [NMEM, D])
    cca = din("cca", [2, 2 * NSB, D])
    ccf = din("ccf", [2, 2 * NSB, 2 * DFF])
    cmk = din("cmk", [2, NSB, NMEM, D])
    cmv = din("cmv", [2, NSB, NMEM, D])
    w_in = din("w_in", [2, D, DIN])
    w_k = din("w_k", [2, D, D])
    w_v = din("w_v", [2, D, D])
    w_o = din("w_o", [2, D, D])
    w_up = din("w_up", [2, D, 2 * DFF])
    w_down = din("w_down", [2, DFF, D])
    sg_w = din("sg_w", [2, 4, 128, 128])
    sg_b = din("sg_b", [2, 4, 128])
    vec1 = din("vec1", [NV1, D])
    vec2 = din("vec2", [NV2, 2 * DFF])

    y_p = dout("y_p", [SEQ, D])
    y_s = dout("y_s", [NST, D])
    o_ca_p = dout("o_ca_p", [2, 2, D])
    o_cf_p = dout("o_cf_p", [2, 2, 2 * DFF])
    o_mk = dout("o_mk", [2, NMEM, D])
    o_mv = dout("o_mv", [2, NMEM, D])
    o_ca_s = dout("o_ca_s", [2, 2 * NSB, D])
    o_cf_s = dout("o_cf_s", [2, 2 * NSB, 2 * DFF])
    o_sv = dout("o_sv", [2, NST, D])

    WMAX = 768
    NTMAX = 6
    import contextlib
    es = contextlib.ExitStack()

    def sb(name, shape, dt):
        return es.enter_context(nc.sbuf_tensor(name, list(shape), dt))

    with es:
        PS = es.enter_context(nc.psum_tensor("ps", [128, 8, 512], F32))
        xres = sb("xres", [128, NTMAX, D], F32)
        zT = sb("zT", [128, 8, WMAX], BF16)
        scr = sb("scr", [128, 8 * WMAX + NTMAX * D], BF16)
        mixT = scr[:, 0:8 * WMAX].rearrange("p (k w) -> p k w", k=8)
        VN0 = 8 * WMAX
        vn = scr[:, VN0:VN0 + NTMAX * D].rearrange("p (t f) -> p t f", t=NTMAX)
        actq = [scr[:, q * 4 * WMAX:(q + 1) * 4 * WMAX].rearrange("p (k w) -> p k w", k=4) for q in range(2)]
        mnT = scr[:, VN0:VN0 + 2048].rearrange("p (k m) -> p k m", k=8)
        qT_t = scr[:, VN0 + 2048:VN0 + 4096].rearrange("p (r i w) -> p r i w", r=2, i=2)
        eT_t = scr[:, VN0 + 4096:VN0 + 6144].rearrange("p (r i w) -> p r i w", r=2, i=2)
        NWS = 3
        wring_t = sb("wring", [128, NWS, 8, 512], BF16)
        wring = Ring([wring_t[:, i] for i in range(NWS)])
        wdring_t = sb("wdring", [128, 2, 8, 512], BF16)
        wdring = Ring([wdring_t[:, i] for i in range(2)])
        cv_t = sb("cv", [128, 4, 512], F32)
        cv_ring = Ring([cv_t[:, i] for i in range(4)])
        KT2 = sb("KT", [128, 2, 8, NMEM], BF16)
        Vb2 = sb("Vb", [128, 2, 2, D], BF16)
        gfin_bc = sb("gfin_bc", [128, D], F32)
        ident_b = sb("ident_b", [128, 128], BF16)
        ident_f = sb("ident_f", [128, 128], F32)
        ones_b = sb("ones_b", [128, 128], BF16)
        ones_f = sb("ones_f", [128, 128], F32)
        WsT = sb("WsT", [128, 2, 4, 128], BF16)
        Wblk = sb("Wblk", [64, 2, 4, 64], BF16)
        Wblk_f = sb("Wblk_f", [64, 2, 4, 64], F32)
        sgb_bc = sb("sgb_bc", [128, 2, 4, 128], F32)
        colv = sb("colv", [128, 8, NV1], F32)
        colf = sb("colf", [128, 44, NV2], F32)
        carry_a = sb("carry_a", [128, 2, 8, 2], F32)
        carry_f = sb("carry_f", [128, 2, 44, 2], F32)
        caT = sb("caT", [128, 8, 2 * NSB], F32)
        cfT = sb("cfT", [128, 44, 2 * NSB], F32)
        st_a = sb("st_a", [128, 8, 34], F32)
        st_f = sb("st_f", [128, 44, 34], F32)
        small = sb("small", [128, 64, 8], F32)
        small_i = [0]

        def sm(n=1):
            i = small_i[0] % 64
            small_i[0] += 1
            return small[:, i, 0:n]

        NTMP = 9
        tmp_t = sb("tmp", [128, NTMP, 520], F32)
        tmp_ring = Ring([tmp_t[:, i] for i in range(NTMP)])

        def tmpf(w=512):
            return tmp_ring.get()[:, 0:w]

        def tmpb(w=1024):
            return tmp_ring.get().bitcast(BF16)[:, 0:w]

        big_t = sb("big", [128, 3, D], F32)
        big_ring = Ring([big_t[:, i] for i in range(3)])
        hbf_t = sb("hbf", [128, 4, WMAX + 4], F32)
        hbf_ring = Ring([hbf_t[:, i] for i in range(4)])
        gm_t = sb("gmr", [128, 4, 512], BF16)
        gm_ring = Ring([gm_t[:, i] for i in range(4)])
        kst_v = wdring_t[:].rearrange("p a k n -> p (a k n)").bitcast(F32).rearrange(
            "p (s m n) -> p s m n", s=8, m=2)
        kst_ring = Ring([kst_v[:, i] for i in range(8)])
        vst_v = hbf_t[:].rearrange("p a n -> p (a n)")[:, 0:3072].rearrange(
            "p (s m n) -> p s m n", s=6, m=2)
        vst_ring = Ring([vst_v[:, i] for i in range(6)])
        ksT_t = sb("ksT", [128, 2, 2, 256], BF16)
        ksT_ring = Ring([ksT_t[:, i] for i in range(2)])
        qT_ring = Ring([qT_t[:, i] for i in range(2)])
        eT_ring = Ring([eT_t[:, i] for i in range(2)])
        eTs = sb("eTs", [128, 2, NST], BF16)
        vbf_t = sb("vbf", [128, 2, 2, 256], BF16)
        vbf_ring = Ring([vbf_t[:, i] for i in range(2)])

        bank_i = [0]
        held = set()

        def banks(n=1):
            s0 = bank_i[0] % 8
            for _ in range(16):
                if s0 + n > 8:
                    s0 = 0
                if any((s0 + q) in held for q in range(n)):
                    s0 += 1
                    continue
                break
            bank_i[0] = s0 + n
            return s0

        def bank(n=1, hold=False):
            s0 = banks(n)
            if hold:
                held.update(range(s0, s0 + n))
            if n == 1:
                return PS[:, s0, :]
            return PS[:, s0:s0 + n, :]

        def release_all():
            held.clear()

        def mm(out, lhsT, rhs, start, stop):
            P.add("pe", lambda e: e.matmul(out, lhsT, rhs, start=start, stop=stop),
                  reads=[lhsT, rhs], writes=[out])

        def tr(out, in_, ident):
            P.add("pe", lambda e: e.transpose(out, in_, ident), reads=[in_, ident], writes=[out])

        def act(out, in_, func, bias=None, scale=None, accum_out=None, eng="act"):
            rd = [in_]
            kw = {}
            if bias is not None:
                kw["bias"] = bias
                if not isinstance(bias, (int, float)):
                    rd.append(bias)
            if scale is not None:
                kw["scale"] = scale
                if not isinstance(scale, (int, float)):
                    rd.append(scale)
            wr = [out]
            if accum_out is not None:
                kw["accum_out"] = accum_out
                wr.append(accum_out)
            P.add("act", lambda e: e.activation(out, in_, func, **kw), reads=rd, writes=wr)

        def tt(out, in0, in1, op, eng="dve"):
            P.add(eng, lambda e: e.tensor_tensor(out, in0, in1, op), reads=[in0, in1], writes=[out])

        def ts(out, in0, s1, op0, s2=None, op1=None, eng="dve"):
            rd = [in0] + [s for s in (s1, s2) if s is not None and not isinstance(s, (int, float))]
            if op1 is None:
                P.add(eng, lambda e: e.tensor_scalar(out, in0, s1, None, op0), reads=rd, writes=[out])
            else:
                P.add(eng, lambda e: e.tensor_scalar(out, in0, s1, s2, op0, op1), reads=rd, writes=[out])

        def stt(out, in0, scalar, in1, op0, op1):
            rd = [in0, in1] + ([] if isinstance(scalar, (int, float)) else [scalar])
            P.add("dve", lambda e: e.scalar_tensor_tensor(out, in0, scalar, in1, op0, op1),
                  reads=rd, writes=[out])

        def cp(out, in_, eng="dve"):
            if eng == "act":
                act(out, in_, AF.Copy)
            else:
                P.add(eng, lambda e: e.tensor_copy(out, in_), reads=[in_], writes=[out])

        def recip(out, in_):
            P.add("dve", lambda e: e.reciprocal(out, in_), reads=[in_], writes=[out])

        def memset(ap, val, eng="dve"):
            P.add(eng, lambda e: e.memset(ap, val), writes=[ap])

        def dma(out, in_, q="sp", **kw):
            P.add(q, lambda e: e.dma_start(out=out, in_=in_, **kw), reads=[in_], writes=[out], dma=True)

        def wload(dst, src):
            dma(dst, src, q="pool", max_dma_last_dim=4096)

        P.phase("const")
        for ti in range(GROUPS[0][1]):
            dma(xres[:, ti, :], xp[GROUPS[0][0] + ti * 128:GROUPS[0][0] + (ti + 1) * 128, :])
        memset(ones_f[:], 1.0)
        cp(ones_b[:], ones_f[:])
        P.add("pool", lambda e: e.affine_select(ident_f[:], ones_f[:], [[-1, 128]], ALU.is_equal, 0.0,
                                                base=0, channel_multiplier=1),
              reads=[ones_f[:]], writes=[ident_f[:]])
        cp(ident_b[:], ident_f[:])
        memset(carry_a[:], 0.0)
        memset(carry_f[:], 0.0)
        memset(Wblk_f[:], 0.0)

        v1st = big_ring.get()
        dma(v1st[0:NV1, :], vec1[:, :])
        pb = bank()
        for k in range(8):
            tr(pb[:, k * NV1:(k + 1) * NV1], v1st[0:NV1, k * 128:(k + 1) * 128], ident_f[0:NV1, 0:NV1])
        cp(colv[:].rearrange("p k v -> p (k v)"), pb[:, 0:8 * NV1])
        for pc in range(11):
            v2st = big_ring.get()
            dma(v2st[0:NV2, 0:512], vec2[:, pc * 512:(pc + 1) * 512])
            pb = bank()
            for k in range(4):
                tr(pb[:, k * NV2:(k + 1) * NV2], v2st[0:NV2, k * 128:(k + 1) * 128], ident_f[0:NV2, 0:NV2])
            cp(colf[:, pc * 4:(pc + 1) * 4, :].rearrange("p k v -> p (k v)"), pb[:, 0:4 * NV2])
        dma(gfin_bc[:], vec1[R_GFIN:R_GFIN + 1, :].partition_broadcast(128))
        sgw_st = big_ring.get().rearrange("p (a s) -> p a s", a=8)
        WsT_f = big_ring.get().rearrange("p (l g t) -> p l g t", l=2, g=4)
        dma(sgw_st, sg_w.rearrange("l g t s -> t (l g) s"))
        for lg in range(8):
            pb = bank()
            tr(pb[:, 0:128], sgw_st[:, lg, :], ident_f[:])
            cp(WsT_f[:, lg // 4, lg % 4, :], pb[:, 0:128])
        P.add("pool", lambda e: e.affine_select(WsT_f[:].rearrange("p l g t -> p (l g) t"),
                                                WsT_f[:].rearrange("p l g t -> p (l g) t"),
                                                [[0, 8], [1, 128]], ALU.is_ge, 0.0,
                                                base=0, channel_multiplier=-1),
              reads=[WsT_f[:]], writes=[WsT_f[:]])
        cp(WsT[:], WsT_f[:])
        for b in range(NSB):
            dma(Wblk_f[4 * b:4 * b + 4, :, :, 4 * b:4 * b + 4], WsT_f[0:4, :, :, 0:4])
        cp(Wblk[:], Wblk_f[:])
        dma(sgb_bc[:].rearrange("p l g t -> p (l g t)"),
            sg_b.rearrange("l g t -> (l g t)").partition_broadcast(128))

        def rms_stats(x_ap, rows):
            junk = tmpb(1024)
            ss = sm()
            act(junk[0:rows, :], x_ap, AF.Square, accum_out=ss[0:rows, :])
            std = sm()
            act(std[0:rows, :], ss[0:rows, :], AF.Sqrt, bias=EPS, scale=1.0 / D)
            rstd = sm()
            recip(rstd[0:rows, :], std[0:rows, :])
            return rstd

        def norm_part1(x_ap, rows):
            rstd = rms_stats(x_ap, rows)
            xsb = tmpb(1024)
            ts(xsb[0:rows, :], x_ap, rstd[0:rows, :], ALU.mult)
            return xsb

        def norm_to_T(x_ap, rows, gcol_idx, dstT, c0, xsb=None):
            if xsb is None:
                xsb = norm_part1(x_ap, rows)
            pbk = bank().bitcast(BF16)
            for k in range(8):
                tr(pbk[:, k * 128:k * 128 + rows], xsb[0:rows, k * 128:(k + 1) * 128],
                   ident_b[0:rows, 0:rows])
            src = pbk.rearrange("p (k t) -> p k t", k=8)[:, :, 0:rows]
            g = colv[:, :, gcol_idx:gcol_idx + 1].to_broadcast([128, 8, rows])
            tt(dstT[:, :, c0:c0 + rows], src, g, ALU.mult)

        for gi, (pstart, npt, has_s) in enumerate(GROUPS):
            Wp = npt * 128
            W = Wp + (NST if has_s else 0)
            tiles = [(t * 128, 128) for t in range(npt)] + ([(Wp, NST)] if has_s else [])
            subs = []
            c = 0
            while c < Wp:
                w = min(512, Wp - c)
                subs.append((c, w, False))
                c += w
            if has_s:
                subs.append((Wp, NST, True))
            last_group = gi == len(GROUPS) - 1

            P.phase("g%d load" % gi)
            for ti, (c0, rows) in enumerate(tiles):
                if gi == 0:
                    continue
                if rows == 128:
                    dma(xres[:, ti, :], xp[pstart + c0:pstart + c0 + 128, :])
                else:
                    dma(xres[0:rows, ti, :], xs[:, :])

            for l in range(2):
                P.phase("g%d l%d kv" % (gi, l))
                KT = KT2[:, l]
                Vb = Vb2[:, l]
                if gi == 0:
                    pass
                    wk = [wring.get(), wring.get()]
                    for u in range(2):
                        wload(wk[u], w_k[l].rearrange("(k p) n -> p k n", p=128)[:, :, u * 512:(u + 1) * 512])
                    P.phase("kv1 memnorm")
                    for mt in range(2):
                        mt_x = big_ring.get()
                        dma(mt_x, mem[mt * 128:(mt + 1) * 128, :])
                        norm_to_T(mt_x, 128, R_GMEM + l, mnT, mt * 128)
                    P.phase("kv2 KT")
                    for dc in range(8):
                        pb = bank()
                        for k in range(8):
                            mm(pb[:, 0:NMEM], wk[dc // 4][:, k, (dc % 4) * 128:(dc % 4 + 1) * 128],
                               mnT[:, k, :], k == 0, k == 7)
                        cp(KT[:, dc, :], pb[:, 0:NMEM], eng="act")
                    P.phase("kv3 Ktok")
                    if gi == 0:
                        for mt in range(2):
                            pb2 = bank(2)
                            for u in range(2):
                                for k in range(8):
                                    mm(pb2[:, u, :], mnT[:, k, mt * 128:(mt + 1) * 128], wk[u][:, k, :],
                                       k == 0, k == 7)
                            ko = big_ring.get()
                            cp(ko.rearrange("p (u n) -> p u n", u=2), pb2, eng="act")
                            dma(o_mk[l, mt * 128:(mt + 1) * 128, :], ko)
                    P.phase("kv4 V")
                    wv = [wring.get(), wring.get()]
                    for u in range(2):
                        wload(wv[u], w_v[l].rearrange("(k p) n -> p k n", p=128)[:, :, u * 512:(u + 1) * 512])
                    for mt in range(2):
                        pb2 = bank(2)
                        for u in range(2):
                            for k in range(8):
                                mm(pb2[:, u, :], mnT[:, k, mt * 128:(mt + 1) * 128], wv[u][:, k, :],
                                   k == 0, k == 7)
                        cp(Vb[:, mt, :].rearrange("p (u n) -> p u n", u=2), pb2, eng="act")
                        if gi == 0:
                            vo = big_ring.get()
                            cp(vo.rearrange("p (u n) -> p u n", u=2), pb2)
                            dma(o_mv[l, mt * 128:(mt + 1) * 128, :], vo)

                P.phase("g%d l%d cache" % (gi, l))
                if has_s:
                    cst = big_ring.get()
                    dma(cst[0:32, :], cca[l])
                    pb = bank()
                    for k in range(8):
                        tr(pb[:, k * 32:(k + 1) * 32], cst[0:32, k * 128:(k + 1) * 128], ident_f[0:32, 0:32])
                    cp(caT[:].rearrange("p k r -> p (k r)"), pb[:, 0:256])
                    for pc in range(11):
                        cst = big_ring.get()
                        dma(cst[0:32, 0:512], ccf[l, :, pc * 512:(pc + 1) * 512])
                        pb = bank()
                        for k in range(4):
                            tr(pb[:, k * 32:(k + 1) * 32], cst[0:32, k * 128:(k + 1) * 128],
                               ident_f[0:32, 0:32])
                        cp(cfT[:, pc * 4:(pc + 1) * 4, :].rearrange("p k r -> p (k r)"), pb[:, 0:128])

                P.phase("g%d l%d norm" % (gi, l))
                nx = {}
                for t0 in range(min(2, len(tiles))):
                    nx[t0] = norm_part1(xres[0:tiles[t0][1], t0, :], tiles[t0][1])
                norm_to_T(None, tiles[0][1], R_GMIX + l, zT, tiles[0][0], xsb=nx.pop(0))

                win = w_in[l].rearrange("(k p) n -> p k n", p=128)

                P.phase("g%d l%d v" % (gi, l))
                wvv = [wring.get(), wring.get()]
                for u in range(2):
                    wload(wvv[u], win[:, :, OFF_V + u * 512:OFF_V + (u + 1) * 512])
                for ti, (c0, rows) in enumerate(tiles):
                    if ti + 2 < len(tiles):
                        nx[ti + 2] = norm_part1(xres[0:tiles[ti + 2][1], ti + 2, :], tiles[ti + 2][1])
                    if ti + 1 < len(tiles):
                        nc0, nrows = tiles[ti + 1]
                        norm_to_T(None, nrows, R_GMIX + l, zT, nc0, xsb=nx.pop(ti + 1))
                    pb2 = bank(2)
                    for u in range(2):
                        for k in range(8):
                            mm(pb2[0:rows, u, :], zT[:, k, c0:c0 + rows], wvv[u][:, k, :], k == 0, k == 7)
                    statbuf = tmpf(16)
                    for u in range(2):
                        P.add("dve", (lambda o, i: (lambda e: e.bn_stats(o, i)))(
                            statbuf[0:rows, u * 6:(u + 1) * 6], pb2[0:rows, u, :]),
                            reads=[pb2[0:rows, u, :]], writes=[statbuf[0:rows, u * 6:(u + 1) * 6]])
                    mv = sm(2)
                    P.add("dve", (lambda o, i: (lambda e: e.bn_aggr(o, i)))(mv[0:rows, :], statbuf[0:rows, 0:12]),
                          reads=[statbuf[0:rows, 0:12]], writes=[mv[0:rows, :]])
                    std = sm()
                    act(std[0:rows, :], mv[0:rows, 1:2], AF.Sqrt, bias=EPS, scale=1.0)
                    rstd = sm()
                    recip(rstd[0:rows, :], std[0:rows, :])
                    nmr = sm()
                    ts(nmr[0:rows, :], mv[0:rows, 0:1], rstd[0:rows, :], ALU.mult, -1.0, ALU.mult)
                    vsrc = pb2[0:rows].rearrange("p u n -> p (u n)")
                    if rows == 128:
                        act(vn[:, ti, :], vsrc, AF.Identity, bias=nmr[0:rows, :], scale=rstd[0:rows, :])
                    else:
                        vh = big_ring.get()
                        act(vh[0:rows, :], vsrc, AF.Identity, bias=nmr[0:rows, :], scale=rstd[0:rows, :])
                        cp(vn[0:rows, ti, :], vh[0:rows, :])
                        gsg_bc = big_ring.get()
                        dma(gsg_bc[0:rows, :], vec1[R_GSG + l:R_GSG + l + 1, :].partition_broadcast(rows))
                        vo = big_ring.get()
                        tt(vo[0:rows, :], vh[0:rows, :], gsg_bc[0:rows, :], ALU.mult)
                        dma(o_sv[l], vo[0:rows, :])

                P.phase("g%d l%d sgu" % (gi, l))
                for jp in range(4):
                    wt = wring.get()
                    wload(wt[:, :, 0:256], win[:, :, OFF_U + jp * 256:OFF_U + (jp + 1) * 256])
                    wload(wt[:, :, 256:512], win[:, :, OFF_GB + jp * 256:OFF_GB + (jp + 1) * 256])
                    g = jp
                    for jj in range(2):
                        j = 2 * jp + jj
                        for (c0, w, is_s) in subs:
                            pbs = bank()
                            if not is_s:
                                for ci in range(w // 128):
                                    ti = (c0 // 128) + ci
                                    mm(pbs[:, ci * 128:(ci + 1) * 128], vn[:, ti, j * 128:(j + 1) * 128],
                                       WsT[:, l, g, :], True, True)
                            else:
                                ti = len(tiles) - 1
                                mm(pbs[:, 0:w], vn[0:NST, ti, j * 128:(j + 1) * 128], Wblk[:, l, g, :],
                                   True, True)
                            pbu = bank()
                            pbg = bank()
                            for k in range(8):
                                mm(pbu[:, 0:w], wt[:, k, jj * 128:(jj + 1) * 128], zT[:, k, c0:c0 + w],
                                   k == 0, k == 7)
                            for k in range(8):
                                mm(pbg[:, 0:w], wt[:, k, 256 + jj * 128:256 + (jj + 1) * 128],
                                   zT[:, k, c0:c0 + w], k == 0, k == 7)
                            t1 = tmpf(w)
                            gsg = colv[:, j, R_GSG + l:R_GSG + l + 1]
                            if not is_s:
                                n = w // 128
                                bias = sgb_bc[:, l, g, :].unsqueeze(1).to_broadcast([128, n, 128])
                                stt(t1.rearrange("p (n t) -> p n t", n=n),
                                    pbs[:, 0:w].rearrange("p (n t) -> p n t", n=n), gsg, bias,
                                    ALU.mult, ALU.add)
                            else:
                                bias = sgb_bc[:, l, g, 0:4].unsqueeze(1).to_broadcast([128, NSB, 4])
                                stt(t1.rearrange("p (n t) -> p n t", n=NSB),
                                    pbs[:, 0:w].rearrange("p (n t) -> p n t", n=NSB), gsg, bias,
                                    ALU.mult, ALU.add)
                            t2 = tmpf(w)
                            tt(t2, t1, pbu[:, 0:w], ALU.mult)
                            gb = tmpb(w)
                            act(gb, pbg[:, 0:w], AF.Sigmoid)
                            tt(mixT[:, j, c0:c0 + w], t2, gb, ALU.mult)

                P.phase("g%d l%d A" % (gi, l))
                for j in range(8):
                    wt = wring.get()
                    for si, off in enumerate((OFF_H, OFF_C, OFF_B, OFF_GA)):
                        wload(wt[:, :, si * 128:(si + 1) * 128], win[:, :, off + j * 128:off + (j + 1) * 128])
                    cw = [colv[:, j, R_CAW + 3 * l + i:R_CAW + 3 * l + i + 1] for i in range(3)]
                    chf = hbf_ring.get()
                    cp(chf[:, 0:2], carry_a[:, l, j, :])
                    for (c0, w, is_s) in subs:
                        pb4 = [bank() for _ in range(4)]
                        for si in range(4):
                            for k in range(8):
                                mm(pb4[si][:, 0:w], wt[:, k, si * 128:(si + 1) * 128], zT[:, k, c0:c0 + w],
                                   k == 0, k == 7)
                        hS = tmpb(w)
                        cp(hS, pb4[0][:, 0:w], eng="act")
                        if not is_s:
                            tt(chf[:, 2 + c0:2 + c0 + w], pb4[1][:, 0:w], hS, ALU.mult)
                            taps = [chf[:, c0 + i:c0 + i + w] for i in range(3)]
                            shp = lambda a: a
                            if c0 + w == Wp:
                                cp(carry_a[:, l, j, :], chf[:, Wp:Wp + 2])
                                if last_group:
                                    cp(st_a[:, j, 0:2], chf[:, Wp:Wp + 2])
                        else:
                            chb = tmp_ring.get()
                            ch3 = chb[:, 0:6 * NSB].rearrange("p (b t) -> p b t", b=NSB)
                            cp(ch3[:, :, 0:2], caT[:, j, :].rearrange("p (b r) -> p b r", b=NSB))
                            tt(ch3[:, :, 2:6], pb4[1][:, 0:w].rearrange("p (b t) -> p b t", b=NSB),
                               hS.rearrange("p (b t) -> p b t", b=NSB), ALU.mult)
                            cp(st_a[:, j, 2:34].rearrange("p (b r) -> p b r", b=NSB), ch3[:, :, 4:6])
                            taps = [ch3[:, :, i:i + 4] for i in range(3)]
                            shp = lambda a: a.rearrange("p (b t) -> p b t", b=NSB)
                        t1 = tmpf(w)
                        ts(shp(t1), taps[0], cw[0], ALU.mult)
                        t2 = tmpf(w)
                        stt(shp(t2), taps[1], cw[1], shp(t1), ALU.mult, ALU.add)
                        t3 = tmpf(w)
                        stt(shp(t3), taps[2], cw[2], shp(t2), ALU.mult, ALU.add)
                        ya = tmpf(w)
                        tt(ya, t3, pb4[2][:, 0:w], ALU.mult)
                        ga = tmpb(w)
                        act(ga, pb4[3][:, 0:w], AF.Sigmoid)
                        yg = tmpf(w)
                        tt(yg, ya, ga, ALU.mult)
                        tt(mixT[:, j, c0:c0 + w], mixT[:, j, c0:c0 + w], yg, ALU.add)

                P.phase("g%d l%d M" % (gi, l))
                def m_stage1(h, wt, c0, w, is_s):
                    qT = qT_ring.get()
                    gm = []
                    for i in range(2):
                        pbq = bank()
                        for k in range(8):
                            mm(pbq[:, 0:w], wt[:, k, i * 128:(i + 1) * 128], zT[:, k, c0:c0 + w],
                               k == 0, k == 7)
                        cp(qT[:, i, 0:w], pbq[:, 0:w], eng="act")
                    for i in range(2):
                        pbg = bank()
                        for k in range(8):
                            mm(pbg[:, 0:w], wt[:, k, 256 + i * 128:256 + (i + 1) * 128],
                               zT[:, k, c0:c0 + w], k == 0, k == 7)
                        gmi = gm_ring.get()[:, 0:w]
                        act(gmi, pbg[:, 0:w], AF.Sigmoid)
                        gm.append(gmi)
                    if not is_s:
                        pbS = bank(2)
                        for mc in range(2):
                            for i in range(2):
                                mm(pbS[:, mc, 0:w], KT[:, 2 * h + i, mc * 128:(mc + 1) * 128],
                                   qT[:, i, 0:w], i == 0, i == 1)
                        eT = eT_ring.get()
                        act(eT[:, :, 0:w], pbS[:, :, 0:w], AF.Exp, scale=1.0 / 16.0)
                        return (h, c0, w, is_s, gm, eT)
                    pbS = bank(hold=True)
                    pbSv = pbS[:, 0:2 * NST].rearrange("p (m t) -> p m t", m=2)
                    pbO_ = bank(hold=True)
                    pbOv = pbO_[:, 0:2 * NST].rearrange("p (i t) -> p i t", i=2)
                    def k_prep(b):
                        kst = kst_ring.get()
                        dma(kst, cmk[l, b].rearrange("(mc p) n -> p mc n", p=128)[:, :, h * 256:(h + 1) * 256])
                        pbt = bank()
                        for i in range(2):
                            for mc in range(2):
                                tr(pbt[:, i * 256 + mc * 128:i * 256 + (mc + 1) * 128],
                                   kst[:, mc, i * 128:(i + 1) * 128], ident_f[:])
                        ksT = ksT_ring.get()
                        cp(ksT.rearrange("p i m -> p (i m)"), pbt[:, 0:512], eng=("act" if b % 2 == 0 else "dve"))
                        return ksT

                    NVPRE = 6
                    v_tiles = {}
                    for b in range(NVPRE):
                        vst = vst_ring.get()
                        dma(vst, cmv[l, b].rearrange("(mc p) n -> p mc n", p=128)[:, :, h * 256:(h + 1) * 256])
                        v_tiles[b] = vst
                    ks_next = k_prep(0)
                    for b in range(NSB):
                        ksT = ks_next
                        if b + 1 < NSB:
                            ks_next = k_prep(b + 1)
                        for mc in range(2):
                            for i in range(2):
                                mm(pbSv[:, mc, 4 * b:4 * b + 4], ksT[:, i, mc * 128:(mc + 1) * 128],
                                   qT[:, i, 4 * b:4 * b + 4], i == 0, i == 1)
                    act(eTs[:], pbSv, AF.Exp, scale=1.0 / 16.0)
                    pbD = bank()
                    for mc in range(2):
                        mm(pbD[:, 0:w], ones_b[:], eTs[:, mc, :], mc == 0, mc == 1)
                    for b in range(NSB):
                        if b in v_tiles:
                            vst = v_tiles[b]
                        else:
                            vst = vst_ring.get()
                            dma(vst, cmv[l, b].rearrange("(mc p) n -> p mc n", p=128)[:, :, h * 256:(h + 1) * 256])
                        vb = vbf_ring.get()
                        cp(vb, vst, eng=("dve" if b % 2 == 0 else "act"))
                        for i in range(2):
                            for mc in range(2):
                                mm(pbOv[:, i, 4 * b:4 * b + 4], vb[:, mc, i * 128:(i + 1) * 128],
                                   eTs[:, mc, 4 * b:4 * b + 4], mc == 0, mc == 1)
                    release_all()
                    m_finish(h, c0, w, gm, pbD, [pbOv[:, 0, :], pbOv[:, 1, :]])
                    return None

                def m_finish(h, c0, w, gm, pbD, pbO):
                    rden = tmpf(w)
                    recip(rden, pbD[:, 0:w])
                    for i in range(2):
                        rg = tmpf(w)
                        tt(rg, rden, gm[i], ALU.mult)
                        om = tmpf(w)
                        tt(om, pbO[i], rg, ALU.mult)
                        dst = mixT[:, 2 * h + i, c0:c0 + w]
                        tt(dst, dst, om, ALU.add)

                def m_stage2(st):
                    (h, c0, w, is_s, gm, eT) = st
                    pbD = bank()
                    for mc in range(2):
                        mm(pbD[:, 0:w], ones_b[:], eT[:, mc, 0:w], mc == 0, mc == 1)
                    pbO = []
                    for i in range(2):
                        pbo = bank()
                        for mc in range(2):
                            mm(pbo[:, 0:w], Vb[:, mc, (2 * h + i) * 128:(2 * h + i + 1) * 128],
                               eT[:, mc, 0:w], mc == 0, mc == 1)
                        pbO.append(pbo[:, 0:w])
                    m_finish(h, c0, w, gm, pbD, pbO)

                pend = None
                for h in range(4):
                    wt = wring.get()
                    wload(wt[:, :, 0:256], win[:, :, OFF_Q + h * 256:OFF_Q + (h + 1) * 256])
                    wload(wt[:, :, 256:512], win[:, :, OFF_GM + h * 256:OFF_GM + (h + 1) * 256])
                    for (c0, w, is_s) in subs:
                        st = m_stage1(h, wt, c0, w, is_s)
                        if pend is not None:
                            m_stage2(pend)
                        pend = st
                if pend is not None:
                    m_stage2(pend)

                P.phase("g%d l%d wo" % (gi, l))
                wo = [wring.get(), wring.get()]
                for u in range(2):
                    wload(wo[u], w_o[l].rearrange("(k p) n -> p k n", p=128)[:, :, u * 512:(u + 1) * 512])
                for ti, (c0, rows) in enumerate(tiles):
                    pb2 = bank(2)
                    for u in range(2):
                        for k in range(8):
                            mm(pb2[0:rows, u, :], mixT[:, k, c0:c0 + rows], wo[u][:, k, :], k == 0, k == 7)
                    xr = xres[0:rows, ti, :]
                    tt(xr.rearrange("p (u n) -> p u n", u=2), pb2[0:rows], xr.rearrange("p (u n) -> p u n", u=2),
                       ALU.add)
                    nxw = norm_part1(xr, rows)
                    if ti > 0:
                        pc0, prows = tiles[ti - 1]
                        norm_to_T(None, prows, R_GFFN + l, zT, pc0, xsb=prev_xsb)
                    prev_xsb = nxw
                pc0, prows = tiles[-1]
                norm_to_T(None, prows, R_GFFN + l, zT, pc0, xsb=prev_xsb)

                P.phase("g%d l%d ffn" % (gi, l))
                wup = w_up[l].rearrange("(k p) n -> p k n", p=128)
                wdn = w_down[l].rearrange("(k p) n -> p k n", p=128)
                cfw = lambda jx, i: colf[:, jx, 4 * l + i:4 * l + i + 1]
                def ffn_up(qi, k0, nk, down_it=None, per_head=1):
                    wdslot = wdring.get()
                    wd = wdslot.rearrange("p (a b) n -> p a (b n)", b=2)
                    actT = actq[qi % 2]
                    for jp in range(k0 // 2, (k0 + nk) // 2):
                        wt = wring.get()
                        wload(wt[:, :, 0:256], wup[:, :, jp * 256:(jp + 1) * 256])
                        wload(wt[:, :, 256:512], wup[:, :, DFF + jp * 256:DFF + (jp + 1) * 256])
                        for jj in range(2):
                            j = 2 * jp + jj
                            hbs = [hbf_ring.get(), hbf_ring.get()]
                            for part in range(2):
                                cp(hbs[part][:, 0:2], carry_f[:, l, part * 22 + j, :])
                            for (c0, w, is_s) in subs:
                                conv = []
                                for part in range(2):
                                    jx = part * 22 + j
                                    pbx = bank()
                                    for k in range(8):
                                        mm(pbx[:, 0:w], wt[:, k, part * 256 + jj * 128:part * 256 + (jj + 1) * 128],
                                           zT[:, k, c0:c0 + w], k == 0, k == 7)
                                    t1 = tmpf(w)
                                    act(t1, pbx[:, 0:w], AF.Identity, bias=cfw(jx, 3), scale=cfw(jx, 2))
                                    if not is_s:
                                        hb = hbs[part]
                                        cp(hb[:, 2 + c0:2 + c0 + w], pbx[:, 0:w], eng="act")
                                        taps = [hb[:, c0 + i:c0 + i + w] for i in range(2)]
                                        shp = lambda a: a
                                        if c0 + w == Wp:
                                            cp(carry_f[:, l, jx, :], hb[:, Wp:Wp + 2])
                                            if last_group:
                                                cp(st_f[:, jx, 0:2], hb[:, Wp:Wp + 2])
                                    else:
                                        hb = tmp_ring.get()
                                        h3 = hb[:, 0:6 * NSB].rearrange("p (b t) -> p b t", b=NSB)
                                        cp(h3[:, :, 0:2], cfT[:, jx, :].rearrange("p (b r) -> p b r", b=NSB))
                                        cp(h3[:, :, 2:6], pbx[:, 0:w].rearrange("p (b t) -> p b t", b=NSB),
                                           eng="act")
                                        cp(st_f[:, jx, 2:34].rearrange("p (b r) -> p b r", b=NSB), h3[:, :, 4:6])
                                        taps = [h3[:, :, i:i + 4] for i in range(2)]
                                        shp = lambda a: a.rearrange("p (b t) -> p b t", b=NSB)
                                    t2 = tmpf(w)
                                    stt(shp(t2), taps[1], cfw(jx, 1), shp(t1), ALU.mult, ALU.add)
                                    t3 = cv_ring.get()[:, 0:w]
                                    stt(shp(t3), taps[0], cfw(jx, 0), shp(t2), ALU.mult, ALU.add)
                                    conv.append(t3)
                                ffn_tail()
                                ffn_pend.append((conv, actT[:, j - k0, c0:c0 + w], w))
                                if down_it is not None:
                                    for _ in range(per_head):
                                        next(down_it, None)
                    wload(wd[:, 0:nk, :], wdn[:, k0:k0 + nk, :])
                    if down_it is not None:
                        for _ in down_it:
                            pass
                    return (actT, wd, nk)

                ffn_pend = []

                def ffn_tail():
                    if ffn_pend:
                        conv, dst, w = ffn_pend.pop()
                        sg = tmpf(w)
                        act(sg, conv[1], AF.Silu)
                        tt(dst, conv[0], sg, ALU.mult)

                def ffn_down_iter(st):
                    (actT, wd, nk) = st
                    for ti, (c0, rows) in enumerate(tiles):
                        yield ffn_down_tile(actT, wd, nk, ti, c0, rows)

                def ffn_down_tile(actT, wd, nk, ti, c0, rows):
                    if True:
                        pb2 = bank(2)
                        for u in range(2):
                            for kk in range(nk):
                                mm(pb2[0:rows, u, :], actT[:, kk, c0:c0 + rows], wd[:, kk, u * 512:(u + 1) * 512],
                                   kk == 0, kk == nk - 1)
                        xr = xres[0:rows, ti, :]
                        tt(xr.rearrange("p (u n) -> p u n", u=2), pb2[0:rows],
                           xr.rearrange("p (u n) -> p u n", u=2), ALU.add)

                pend = None
                for qi, (k0, nk) in enumerate(((0, 4), (4, 4), (8, 4), (12, 4), (16, 4), (20, 2))):
                    nheads = nk * len(subs)
                    if pend is not None:
                        st = ffn_up(qi, k0, nk, ffn_down_iter(pend), -(-len(tiles) // nheads))
                    else:
                        st = ffn_up(qi, k0, nk)
                    pend = st
                ffn_tail()
                for _ in ffn_down_iter(pend):
                    pass

                P.phase("g%d l%d state" % (gi, l))
                if last_group:
                    for (st, nch, o_p, o_s) in ((st_a, 8, o_ca_p, o_ca_s), (st_f, 44, o_cf_p, o_cf_s)):
                        for pc in range(nch // 4):
                            pb = bank()
                            for k in range(4):
                                tr(pb[0:34, k * 128:(k + 1) * 128], st[:, pc * 4 + k, :], ident_f[:])
                            so = big_ring.get()
                            cp(so[0:34, 0:512], pb[0:34, :], eng="act")
                            dma(o_p[l, :, pc * 512:(pc + 1) * 512], so[0:2, 0:512])
                            dma(o_s[l, :, pc * 512:(pc + 1) * 512], so[2:34, 0:512])

            P.phase("g%d final" % gi)
            for ti, (c0, rows) in enumerate(tiles):
                xr = xres[0:rows, ti, :]
                rstd = rms_stats(xr, rows)
                yo = big_ring.get()
                stt(yo[0:rows, :], xr, rstd[0:rows, :], gfin_bc[0:rows, :], ALU.mult, ALU.mult)
                if rows == 128:
                    dma(y_p[pstart + c0:pstart + c0 + 128, :], yo)
                else:
                    dma(y_s[:, :], yo[0:rows, :])

        P.finalize()
        csems = {e: es.enter_context(nc.semaphore("c_" + e)) for e in Prog.ENGS}
        dsems = {e: [es.enter_context(nc.semaphore("d_%s_%d" % (e, i))) for i in range(Prog.DMA_POOL)]
                 for e in ("sp", "pool", "act")}
        with nc.allow_low_precision("bf16 matmul operands, fp32 accumulation"), \
                nc.allow_non_contiguous_dma("small strided constant loads"):
            with nc.Block() as block:
                P.emit(nc, block, csems, dsems)
    return nc, P


_CACHE = {}


def kernel(**inputs):
    f = lambda a: np.ascontiguousarray(np.asarray(a, dtype=np.float32))
    x_prompt = f(inputs["x_prompt"])
    x_sample = f(inputs["x_sample"])
    mem_prompt = f(inputs["mem_prompt"])
    cache_conv_a = f(inputs["cache_conv_a"])
    cache_conv_ffn = f(inputs["cache_conv_ffn"])
    cache_mem_k = f(inputs["cache_mem_k"])
    cache_mem_v = f(inputs["cache_mem_v"])
    vec1 = np.ascontiguousarray(np.concatenate([
        f(inputs["norm_mix_g"]), f(inputs["norm_ffn_g"]), f(inputs["norm_mem_g"]), f(inputs["sg_ln_g"]),
        f(inputs["norm_final_g"])[None, :], f(inputs["conv_a_w"]).reshape(6, D)], axis=0))
    cfw = f(inputs["conv_f_w"])
    cfb = f(inputs["conv_f_b"])
    vec2 = np.ascontiguousarray(np.concatenate([cfw[0], cfb[0][None], cfw[1], cfb[1][None]], axis=0))
    shared = {
        "w_in": f(inputs["w_in"]), "w_k": f(inputs["w_k"]), "w_v": f(inputs["w_v"]), "w_o": f(inputs["w_o"]),
        "w_up": f(inputs["w_up"]), "w_down": f(inputs["w_down"]), "sg_w": f(inputs["sg_w"]),
        "sg_b": f(inputs["sg_b"]), "vec1": vec1, "vec2": vec2,
    }
    in_maps = []
    for c in range(NCORES):
        bs = slice(c * NSB, (c + 1) * NSB)
        m = dict(shared)
        m["xp"] = x_prompt[c]
        m["xs"] = np.ascontiguousarray(x_sample[bs].reshape(NST, D))
        m["mem"] = mem_prompt[c]
        m["cca"] = np.ascontiguousarray(cache_conv_a[:, bs].reshape(2, 2 * NSB, D))
        m["ccf"] = np.ascontiguousarray(cache_conv_ffn[:, bs].reshape(2, 2 * NSB, 2 * DFF))
        m["cmk"] = np.ascontiguousarray(cache_mem_k[:, bs].reshape(2, NSB, NMEM, D))
        m["cmv"] = np.ascontiguousarray(cache_mem_v[:, bs].reshape(2, NSB, NMEM, D))
        in_maps.append(m)
    if "nc" not in _CACHE:
        _CACHE["nc"] = build_program()[0]
    nc = _CACHE["nc"]
    res = run_bass_kernel_spmd(nc, in_maps, core_ids=list(range(NCORES)))
    R = res.results
    st = lambda name: np.stack([np.asarray(R[c][name], dtype=np.float32) for c in range(NCORES)], axis=0)
    y_prompt = st("y_p")
    y_sample = st("y_s").reshape(NCORES * NSB, 4, D)
    ca_p = st("o_ca_p").transpose(1, 0, 2, 3)
    cf_p = st("o_cf_p").transpose(1, 0, 2, 3)
    mk = st("o_mk").transpose(1, 0, 2, 3).reshape(2, NCORES, NMEM, 4, 256)
    mv = st("o_mv").transpose(1, 0, 2, 3).reshape(2, NCORES, NMEM, 4, 256)
    ca_s = st("o_ca_s").reshape(NCORES, 2, NSB, 2, D).transpose(1, 0, 2, 3, 4).reshape(2, NCORES * NSB, 2, D)
    cf_s = st("o_cf_s").reshape(NCORES, 2, NSB, 2, 2 * DFF).transpose(1, 0, 2, 3, 4).reshape(
        2, NCORES * NSB, 2, 2 * DFF)
    sv = st("o_sv").reshape(NCORES, 2, NSB, 4, D).transpose(1, 0, 2, 3, 4).reshape(2, NCORES * NSB, 4, D)
    return tuple(np.ascontiguousarray(a) for a in (y_prompt, y_sample, ca_p, cf_p, mk, mv, ca_s, cf_s, sv))
```

```python
import itertools
import os
import numpy as np
import concourse.bass as bass
import concourse.mybir as mybir
from concourse.bass_utils import run_bass_kernel_spmd

F32 = mybir.dt.float32
BF16 = mybir.dt.bfloat16
AF = mybir.ActivationFunctionType
ALU = mybir.AluOpType
SZ = {F32: 4, BF16: 2}

D = 1024
DFF = 2816
DIN = 9216
NMEM = 256
EPS = 1e-6
NCORES = 8
SEQ = 2048
NSB = 16
NST = 64

OFF_H, OFF_C, OFF_B, OFF_U, OFF_V, OFF_Q, OFF_GA, OFF_GB, OFF_GM = (
    0, 1024, 2048, 3072, 4096, 5120, 6144, 7168, 8192)

R_GMIX, R_GFFN, R_GMEM, R_GSG, R_GFIN, R_CAW = 0, 2, 4, 6, 8, 9
NV1 = 15
NV2 = 8

GROUPS = [(0, 6, False), (768, 6, False), (1536, 4, True)]


def _intervals(ap):
    esz = SZ[ap.dtype]
    pairs = list(ap.ap)
    pstep = pairs[0][0]
    off = ap.offset % pstep if pstep else ap.offset
    free = [(s, c) for s, c in pairs[1:] if c > 1 and s != 0]
    if not free:
        if ap.tensor.name == "ps":
            return [((off * esz // 2048) * 2048, (off * esz // 2048) * 2048 + 2048)]
        return [(off * esz, (off + 1) * esz)]
    if free[-1][0] == 1:
        inner = free[-1][1]
        outer = free[:-1]
    else:
        inner = 1
        outer = free
    n_outer = 1
    for _, c in outer:
        n_outer *= c
    if n_outer > 48:
        hi = off + sum(s * (c - 1) for s, c in free) + 1
        if ap.tensor.name == "ps":
            return [((off * esz // 2048) * 2048, ((hi * esz + 2047) // 2048) * 2048)]
        return [(off * esz, hi * esz)]
    res = []
    for idx in itertools.product(*[range(c) for _, c in outer]):
        o = off + sum(i * s for i, (s, _) in zip(idx, outer))
        res.append((o * esz, (o + inner) * esz))
    if ap.tensor.name == "ps":
        res = [((lo // 2048) * 2048, ((hi + 2047) // 2048) * 2048) for lo, hi in res]
    res.sort()
    merged = [res[0]]
    for lo, hi in res[1:]:
        if lo <= merged[-1][1]:
            merged[-1] = (merged[-1][0], max(hi, merged[-1][1]))
        else:
            merged.append((lo, hi))
    return merged


def _onchip(ap):
    return "DRAM" not in str(ap.space).upper() and "HBM" not in str(ap.space).upper()


class _Op:
    __slots__ = ("eng", "fn", "deps", "is_dma", "signaled", "seq", "dsem", "dval", "idx", "dprev", "raw")


class Prog:
    ENGS = ("pe", "act", "dve", "pool", "sp")
    DMA_POOL = 24

    def __init__(self):
        self.ops = []
        self.streams = {e: [] for e in self.ENGS}
        self.recs = {}
        self.ndma = {e: 0 for e in self.ENGS}
        self.dma_hist = {e: [] for e in self.ENGS}
        self.enabled = True
        self.phase_i = 0
        self.phase_limit = int(os.environ.get("KSTOP", "1000000"))
        self.phase_log = []

    def phase(self, name):
        self.phase_log.append((self.phase_i, name, len(self.ops)))
        self.phase_i += 1
        if self.phase_i > self.phase_limit:
            self.enabled = False

    def add(self, eng, fn, reads=(), writes=(), dma=False):
        if not self.enabled:
            return None
        op = _Op()
        op.eng, op.fn, op.is_dma = eng, fn, dma
        op.signaled = False
        op.seq = 0
        op.idx = len(self.ops)
        op.dprev = None
        op.dsem = op.dval = 0
        deps = set()
        for ap in reads:
            if _onchip(ap):
                if ap.tensor.name == "ps":
                    self._write(ap, op, deps)
                else:
                    self._read(ap, op, deps)
        op.raw = set(deps)
        for ap in writes:
            if _onchip(ap):
                self._write(ap, op, deps)
        op.deps = deps
        if dma:
            hist = self.dma_hist[eng]
            i = len(hist)
            op.dsem = i % self.DMA_POOL
            op.dval = 16 * (i // self.DMA_POOL + 1)
            if i >= self.DMA_POOL:
                op.dprev = hist[i - self.DMA_POOL]
            hist.append(op.idx)
            self.ndma[eng] = i + 1
        self.ops.append(op)
        self.streams[eng].append(op)
        return op

    def _read(self, ap, op, deps):
        recs = self.recs.setdefault(ap.tensor.name, [])
        for lo, hi in _intervals(ap):
            hit = False
            for r in recs:
                if r[0] < hi and lo < r[1]:
                    hit = True
                    if r[2] is not None:
                        deps.add(r[2])
                    if op.is_dma:
                        r[4].append(op.idx)
                    else:
                        r[3][op.eng] = op.idx
            if not hit:
                recs.append([lo, hi, None, ({} if op.is_dma else {op.eng: op.idx}),
                             ([op.idx] if op.is_dma else [])])

    def _write(self, ap, op, deps):
        name = ap.tensor.name
        recs = self.recs.setdefault(name, [])
        for lo, hi in _intervals(ap):
            keep = []
            for r in recs:
                if r[0] < hi and lo < r[1]:
                    if r[2] is not None:
                        deps.add(r[2])
                    deps.update(r[3].values())
                    deps.update(r[4])
                    if r[0] < lo:
                        keep.append([r[0], lo, r[2], dict(r[3]), list(r[4])])
                    if hi < r[1]:
                        keep.append([hi, r[1], r[2], dict(r[3]), list(r[4])])
                else:
                    keep.append(r)
            keep.append([lo, hi, op.idx, {}, []])
            recs = keep
        self.recs[name] = recs

    def finalize(self):
        ops = self.ops
        for op in ops:
            need = set()
            for d in op.deps:
                if d == op.idx:
                    continue
                dop = ops[d]
                if dop.is_dma:
                    need.add(d)
                elif dop.eng != op.eng or op.eng != "pe":
                    dop.signaled = True
                    need.add(d)
            if op.dprev is not None:
                need.add(op.dprev)
            op.deps = need
        for e in self.ENGS:
            c = 0
            for op in self.streams[e]:
                if op.signaled and not op.is_dma:
                    c += 1
                    op.seq = c

    def emit(self, nc, block, csems, dsems):
        ops = self.ops
        hooks = {"pe": block.tensor, "act": block.scalar, "dve": block.vector,
                 "pool": block.gpsimd, "sp": block.sync}

        def make(ename):
            def body(e):
                waited = {}
                for op in self.streams[ename]:
                    want = {}
                    for d in op.deps:
                        dop = ops[d]
                        if dop.is_dma:
                            key = ("d", dop.eng, dop.dsem)
                            val = dop.dval
                        else:
                            key = ("c", dop.eng)
                            val = dop.seq
                        if val > want.get(key, 0):
                            want[key] = val
                    for key, val in want.items():
                        if waited.get(key, 0) >= val:
                            continue
                        sem = dsems[key[1]][key[2]] if key[0] == "d" else csems[key[1]]
                        e.wait_ge(sem, val)
                        waited[key] = val
                    inst = op.fn(e)
                    if op.is_dma:
                        inst.then_inc(dsems[ename][op.dsem], 16)
                    elif op.signaled:
                        inst.then_inc(csems[ename], 1)
                n = self.ndma[ename]
                for s in range(min(n, self.DMA_POOL)):
                    cnt = (n - 1 - s) // self.DMA_POOL + 1
                    e.wait_ge(dsems[ename][s], 16 * cnt)
            return body

        for ename in self.ENGS:
            if self.streams[ename]:
                hooks[ename](make(ename))


class Ring:
    def __init__(self, views):
        self.views = views
        self.i = 0

    def get(self):
        v = self.views[self.i % len(self.views)]
        self.i += 1
        return v


def build_program():
    nc = bass.Bass("TRN2", target_bir_lowering=False)
    P = Prog()

    def din(name, shape):
        return nc.dram_tensor(name, list(shape), F32, kind="ExternalInput").ap()

    def dout(name, shape):
        return nc.dram_tensor(name, list(shape), F32, kind="ExternalOutput").ap()

    xp = din("xp", [SEQ, D])
    xs = din("xs", [NST, D])
    mem = din("mem", [NMEM, D])
    cca = din("cca", [2, 2 * NSB, D])
    ccf = din("ccf", [2, 2 * NSB, 2 * DFF])
    cmk = din("cmk", [2, NSB, NMEM, D])
    cmv = din("cmv", [2, NSB, NMEM, D])
    w_in = din("w_in", [2, D, DIN])
    w_k = din("w_k", [2, D, D])
    w_v = din("w_v", [2, D, D])
    w_o = din("w_o", [2, D, D])
    w_up = din("w_up", [2, D, 2 * DFF])
    w_down = din("w_down", [2, DFF, D])
    sg_w = din("sg_w", [2, 4, 128, 128])
    sg_b = din("sg_b", [2, 4, 128])
    vec1 = din("vec1", [NV1, D])
    vec2 = din("vec2", [NV2, 2 * DFF])

    y_p = dout("y_p", [SEQ, D])
    y_s = dout("y_s", [NST, D])
    o_ca_p = dout("o_ca_p", [2, 2, D])
    o_cf_p = dout("o_cf_p", [2, 2, 2 * DFF])
    o_mk = dout("o_mk", [2, NMEM, D])
    o_mv = dout("o_mv", [2, NMEM, D])
    o_ca_s = dout("o_ca_s", [2, 2 * NSB, D])
    o_cf_s = dout("o_cf_s", [2, 2 * NSB, 2 * DFF])
    o_sv = dout("o_sv", [2, NST, D])

    WMAX = 768
    NTMAX = 6
    import contextlib
    es = contextlib.ExitStack()

    def sb(name, shape, dt):
        return es.enter_context(nc.sbuf_tensor(name, list(shape), dt))

    with es:
        PS = es.enter_context(nc.psum_tensor("ps", [128, 8, 512], F32))
        xres = sb("xres", [128, NTMAX, D], F32)
        zT = sb("zT", [128, 8, WMAX], BF16)
        scr = sb("scr", [128, 8 * WMAX + NTMAX * D], BF16)
        mixT = scr[:, 0:8 * WMAX].rearrange("p (k w) -> p k w", k=8)
        VN0 = 8 * WMAX
        vn = scr[:, VN0:VN0 + NTMAX * D].rearrange("p (t f) -> p t f", t=NTMAX)
        actq = [scr[:, q * 4 * WMAX:(q + 1) * 4 * WMAX].rearrange("p (k w) -> p k w", k=4) for q in range(2)]
        mnT = scr[:, VN0:VN0 + 2048].rearrange("p (k m) -> p k m", k=8)
        qT_t = scr[:, VN0 + 2048:VN0 + 4096].rearrange("p (r i w) -> p r i w", r=2, i=2)
        eT_t = scr[:, VN0 + 4096:VN0 + 6144].rearrange("p (r i w) -> p r i w", r=2, i=2)
        NWS = 3
        wring_t = sb("wring", [128, NWS, 8, 512], BF16)
        wring = Ring([wring_t[:, i] for i in range(NWS)])
        wdring_t = sb("wdring", [128, 2, 8, 512], BF16)
        wdring = Ring([wdring_t[:, i] for i in range(2)])
        cv_t = sb("cv", [128, 4, 512], F32)
        cv_ring = Ring([cv_t[:, i] for i in range(4)])
        KT2 = sb("KT", [128, 2, 8, NMEM], BF16)
        Vb2 = sb("Vb", [128, 2, 2, D], BF16)
        gfin_bc = sb("gfin_bc", [128, D], F32)
        ident_b = sb("ident_b", [128, 128], BF16)
        ident_f = sb("ident_f", [128, 128], F32)
        ones_b = sb("ones_b", [128, 128], BF16)
        ones_f = sb("ones_f", [128, 128], F32)
        WsT = sb("WsT", [128, 2, 4, 128], BF16)
        Wblk = sb("Wblk", [64, 2, 4, 64], BF16)
        Wblk_f = sb("Wblk_f", [64, 2, 4, 64], F32)
        sgb_bc = sb("sgb_bc", [128, 2, 4, 128], F32)
        colv = sb("colv", [128, 8, NV1], F32)
        colf = sb("colf", [128, 44, NV2], F32)
        carry_a = sb("carry_a", [128, 2, 8, 2], F32)
        carry_f = sb("carry_f", [128, 2, 44, 2], F32)
        caT = sb("caT", [128, 8, 2 * NSB], F32)
        cfT = sb("cfT", [128, 44, 2 * NSB], F32)
        st_a = sb("st_a", [128, 8, 34], F32)
        st_f = sb("st_f", [128, 44, 34], F32)
        small = sb("small", [128, 64, 8], F32)
        small_i = [0]

        def sm(n=1):
            i = small_i[0] % 64
            small_i[0] += 1
            return small[:, i, 0:n]

        NTMP = 9
        tmp_t = sb("tmp", [128, NTMP, 520], F32)
        tmp_ring = Ring([tmp_t[:, i] for i in range(NTMP)])

        def tmpf(w=512):
            return tmp_ring.get()[:, 0:w]

        def tmpb(w=1024):
            return tmp_ring.get().bitcast(BF16)[:, 0:w]

        big_t = sb("big", [128, 3, D], F32)
        big_ring = Ring([big_t[:, i] for i in range(3)])
        hbf_t = sb("hbf", [128, 4, WMAX + 4], F32)
        hbf_ring = Ring([hbf_t[:, i] for i in range(4)])
        gm_t = sb("gmr", [128, 4, 512], BF16)
        gm_ring = Ring([gm_t[:, i] for i in range(4)])
        kst_v = wdring_t[:].rearrange("p a k n -> p (a k n)").bitcast(F32).rearrange(
            "p (s m n) -> p s m n", s=8, m=2)
        kst_ring = Ring([kst_v[:, i] for i in range(8)])
        vst_v = hbf_t[:].rearrange("p a n -> p (a n)")[:, 0:3072].rearrange(
            "p (s m n) -> p s m n", s=6, m=2)
        vst_ring = Ring([vst_v[:, i] for i in range(6)])
        ksT_t = sb("ksT", [128, 2, 2, 256], BF16)
        ksT_ring = Ring([ksT_t[:, i] for i in range(2)])
        qT_ring = Ring([qT_t[:, i] for i in range(2)])
        eT_ring = Ring([eT_t[:, i] for i in range(2)])
        eTs = sb("eTs", [128, 2, NST], BF16)
        vbf_t = sb("vbf", [128, 2, 2, 256], BF16)
        vbf_ring = Ring([vbf_t[:, i] for i in range(2)])

        bank_i = [0]
        held = set()

        def banks(n=1):
            s0 = bank_i[0] % 8
            for _ in range(16):
                if s0 + n > 8:
                    s0 = 0
                if any((s0 + q) in held for q in range(n)):
                    s0 += 1
                    continue
                break
            bank_i[0] = s0 + n
            return s0

        def bank(n=1, hold=False):
            s0 = banks(n)
            if hold:
                held.update(range(s0, s0 + n))
            if n == 1:
                return PS[:, s0, :]
            return PS[:, s0:s0 + n, :]

        def release_all():
            held.clear()

        def mm(out, lhsT, rhs, start, stop):
            P.add("pe", lambda e: e.matmul(out, lhsT, rhs, start=start, stop=stop),
                  reads=[lhsT, rhs], writes=[out])

        def tr(out, in_, ident):
            P.add("pe", lambda e: e.transpose(out, in_, ident), reads=[in_, ident], writes=[out])

        def act(out, in_, func, bias=None, scale=None, accum_out=None, eng="act"):
            rd = [in_]
            kw = {}
            if bias is not None:
                kw["bias"] = bias
                if not isinstance(bias, (int, float)):
                    rd.append(bias)
            if scale is not None:
                kw["scale"] = scale
                if not isinstance(scale, (int, float)):
                    rd.append(scale)
            wr = [out]
            if accum_out is not None:
                kw["accum_out"] = accum_out
                wr.append(accum_out)
            P.add("act", lambda e: e.activation(out, in_, func, **kw), reads=rd, writes=wr)

        def tt(out, in0, in1, op, eng="dve"):
            P.add(eng, lambda e: e.tensor_tensor(out, in0, in1, op), reads=[in0, in1], writes=[out])

        def ts(out, in0, s1, op0, s2=None, op1=None, eng="dve"):
            rd = [in0] + [s for s in (s1, s2) if s is not None and not isinstance(s, (int, float))]
            if op1 is None:
                P.add(eng, lambda e: e.tensor_scalar(out, in0, s1, None, op0), reads=rd, writes=[out])
            else:
                P.add(eng, lambda e: e.tensor_scalar(out, in0, s1, s2, op0, op1), reads=rd, writes=[out])

        def stt(out, in0, scalar, in1, op0, op1):
            rd = [in0, in1] + ([] if isinstance(scalar, (int, float)) else [scalar])
            P.add("dve", lambda e: e.scalar_tensor_tensor(out, in0, scalar, in1, op0, op1),
                  reads=rd, writes=[out])

        def cp(out, in_, eng="dve"):
            if eng == "act":
                act(out, in_, AF.Copy)
            else:
                P.add(eng, lambda e: e.tensor_copy(out, in_), reads=[in_], writes=[out])

        def recip(out, in_):
            P.add("dve", lambda e: e.reciprocal(out, in_), reads=[in_], writes=[out])

        def memset(ap, val, eng="dve"):
            P.add(eng, lambda e: e.memset(ap, val), writes=[ap])

        def dma(out, in_, q="sp", **kw):
            P.add(q, lambda e: e.dma_start(out=out, in_=in_, **kw), reads=[in_], writes=[out], dma=True)

        def wload(dst, src):
            dma(dst, src, q="pool", max_dma_last_dim=4096)

        P.phase("const")
        for ti in range(GROUPS[0][1]):
            dma(xres[:, ti, :], xp[GROUPS[0][0] + ti * 128:GROUPS[0][0] + (ti + 1) * 128, :])
        memset(ones_f[:], 1.0)
        cp(ones_b[:], ones_f[:])
        P.add("pool", lambda e: e.affine_select(ident_f[:], ones_f[:], [[-1, 128]], ALU.is_equal, 0.0,
                                                base=0, channel_multiplier=1),
              reads=[ones_f[:]], writes=[ident_f[:]])
        cp(ident_b[:], ident_f[:])
        memset(carry_a[:], 0.0)
        memset(carry_f[:], 0.0)
        memset(Wblk_f[:], 0.0)

        v1st = big_ring.get()
        dma(v1st[0:NV1, :], vec1[:, :])
        pb = bank()
        for k in range(8):
            tr(pb[:, k * NV1:(k + 1) * NV1], v1st[0:NV1, k * 128:(k + 1) * 128], ident_f[0:NV1, 0:NV1])
        cp(colv[:].rearrange("p k v -> p (k v)"), pb[:, 0:8 * NV1])
        for pc in range(11):
            v2st = big_ring.get()
            dma(v2st[0:NV2, 0:512], vec2[:, pc * 512:(pc + 1) * 512])
            pb = bank()
            for k in range(4):
                tr(pb[:, k * NV2:(k + 1) * NV2], v2st[0:NV2, k * 128:(k + 1) * 128], ident_f[0:NV2, 0:NV2])
            cp(colf[:, pc * 4:(pc + 1) * 4, :].rearrange("p k v -> p (k v)"), pb[:, 0:4 * NV2])
        dma(gfin_bc[:], vec1[R_GFIN:R_GFIN + 1, :].partition_broadcast(128))
        sgw_st = big_ring.get().rearrange("p (a s) -> p a s", a=8)
        WsT_f = big_ring.get().rearrange("p (l g t) -> p l g t", l=2, g=4)
        dma(sgw_st, sg_w.rearrange("l g t s -> t (l g) s"))
        for lg in range(8):
            pb = bank()
            tr(pb[:, 0:128], sgw_st[:, lg, :], ident_f[:])
            cp(WsT_f[:, lg // 4, lg % 4, :], pb[:, 0:128])
        P.add("pool", lambda e: e.affine_select(WsT_f[:].rearrange("p l g t -> p (l g) t"),
                                                WsT_f[:].rearrange("p l g t -> p (l g) t"),
                                                [[0, 8], [1, 128]], ALU.is_ge, 0.0,
                                                base=0, channel_multiplier=-1),
              reads=[WsT_f[:]], writes=[WsT_f[:]])
        cp(WsT[:], WsT_f[:])
        for b in range(NSB):
            dma(Wblk_f[4 * b:4 * b + 4, :, :, 4 * b:4 * b + 4], WsT_f[0:4, :, :, 0:4])
        cp(Wblk[:], Wblk_f[:])
        dma(sgb_bc[:].rearrange("p l g t -> p (l g t)"),
            sg_b.rearrange("l g t -> (l g t)").partition_broadcast(128))

        def rms_stats(x_ap, rows):
            junk = tmpb(1024)
            ss = sm()
            act(junk[0:rows, :], x_ap, AF.Square, accum_out=ss[0:rows, :])
            std = sm()
            act(std[0:rows, :], ss[0:rows, :], AF.Sqrt, bias=EPS, scale=1.0 / D)
            rstd = sm()
            recip(rstd[0:rows, :], std[0:rows, :])
            return rstd

        def norm_part1(x_ap, rows):
            rstd = rms_stats(x_ap, rows)
            xsb = tmpb(1024)
            act(xsb[0:rows, :], x_ap, AF.Copy, scale=rstd[0:rows, :])
            return xsb

        def norm_to_T(x_ap, rows, gcol_idx, dstT, c0, xsb=None):
            if xsb is None:
                xsb = norm_part1(x_ap, rows)
            pbk = bank().bitcast(BF16)
            for k in range(8):
                tr(pbk[:, k * 128:k * 128 + rows], xsb[0:rows, k * 128:(k + 1) * 128],
                   ident_b[0:rows, 0:rows])
            src = pbk.rearrange("p (k t) -> p k t", k=8)[:, :, 0:rows]
            g = colv[:, :, gcol_idx:gcol_idx + 1].to_broadcast([128, 8, rows])
            tt(dstT[:, :, c0:c0 + rows], src, g, ALU.mult)

        for gi, (pstart, npt, has_s) in enumerate(GROUPS):
            Wp = npt * 128
            W = Wp + (NST if has_s else 0)
            tiles = [(t * 128, 128) for t in range(npt)] + ([(Wp, NST)] if has_s else [])
            subs = []
            c = 0
            while c < Wp:
                w = min(512, Wp - c)
                subs.append((c, w, False))
                c += w
            if has_s:
                subs.append((Wp, NST, True))
            last_group = gi == len(GROUPS) - 1

            P.phase("g%d load" % gi)
            for ti, (c0, rows) in enumerate(tiles):
                if gi == 0:
                    continue
                if rows == 128:
                    dma(xres[:, ti, :], xp[pstart + c0:pstart + c0 + 128, :])
                else:
                    dma(xres[0:rows, ti, :], xs[:, :])

            for l in range(2):
                P.phase("g%d l%d kv" % (gi, l))
                KT = KT2[:, l]
                Vb = Vb2[:, l]
                if gi == 0:
                    pass
                    wk = [wring.get(), wring.get()]
                    for u in range(2):
                        wload(wk[u], w_k[l].rearrange("(k p) n -> p k n", p=128)[:, :, u * 512:(u + 1) * 512])
                    P.phase("kv1 memnorm")
                    for mt in range(2):
                        mt_x = big_ring.get()
                        dma(mt_x, mem[mt * 128:(mt + 1) * 128, :])
                        norm_to_T(mt_x, 128, R_GMEM + l, mnT, mt * 128)
                    P.phase("kv2 KT")
                    for dc in range(8):
                        pb = bank()
                        for k in range(8):
                            mm(pb[:, 0:NMEM], wk[dc // 4][:, k, (dc % 4) * 128:(dc % 4 + 1) * 128],
                               mnT[:, k, :], k == 0, k == 7)
                        cp(KT[:, dc, :], pb[:, 0:NMEM], eng="act")
                    P.phase("kv3 Ktok")
                    if gi == 0:
                        for mt in range(2):
                            pb2 = bank(2)
                            for u in range(2):
                                for k in range(8):
                                    mm(pb2[:, u, :], mnT[:, k, mt * 128:(mt + 1) * 128], wk[u][:, k, :],
                                       k == 0, k == 7)
                            ko = big_ring.get()
                            cp(ko.rearrange("p (u n) -> p u n", u=2), pb2, eng="act")
                            dma(o_mk[l, mt * 128:(mt + 1) * 128, :], ko)
                    P.phase("kv4 V")
                    wv = [wring.get(), wring.get()]
                    for u in range(2):
                        wload(wv[u], w_v[l].rearrange("(k p) n -> p k n", p=128)[:, :, u * 512:(u + 1) * 512])
                    for mt in range(2):
                        pb2 = bank(2)
                        for u in range(2):
                            for k in range(8):
                                mm(pb2[:, u, :], mnT[:, k, mt * 128:(mt + 1) * 128], wv[u][:, k, :],
                                   k == 0, k == 7)
                        cp(Vb[:, mt, :].rearrange("p (u n) -> p u n", u=2), pb2, eng="act")
                        if gi == 0:
                            vo = big_ring.get()
                            cp(vo.rearrange("p (u n) -> p u n", u=2), pb2)
                            dma(o_mv[l, mt * 128:(mt + 1) * 128, :], vo)

                P.phase("g%d l%d cache" % (gi, l))
                if has_s:
                    cst = big_ring.get()
                    dma(cst[0:32, :], cca[l])
                    pb = bank()
                    for k in range(8):
                        tr(pb[:, k * 32:(k + 1) * 32], cst[0:32, k * 128:(k + 1) * 128], ident_f[0:32, 0:32])
                    cp(caT[:].rearrange("p k r -> p (k r)"), pb[:, 0:256])
                    for pc in range(11):
                        cst = big_ring.get()
                        dma(cst[0:32, 0:512], ccf[l, :, pc * 512:(pc + 1) * 512])
                        pb = bank()
                        for k in range(4):
                            tr(pb[:, k * 32:(k + 1) * 32], cst[0:32, k * 128:(k + 1) * 128],
                               ident_f[0:32, 0:32])
                        cp(cfT[:, pc * 4:(pc + 1) * 4, :].rearrange("p k r -> p (k r)"), pb[:, 0:128])

                P.phase("g%d l%d norm" % (gi, l))
                nx = {}
                for t0 in range(min(2, len(tiles))):
                    nx[t0] = norm_part1(xres[0:tiles[t0][1], t0, :], tiles[t0][1])
                norm_to_T(None, tiles[0][1], R_GMIX + l, zT, tiles[0][0], xsb=nx.pop(0))

                win = w_in[l].rearrange("(k p) n -> p k n", p=128)

                P.phase("g%d l%d v" % (gi, l))
                wvv = [wring.get(), wring.get()]
                for u in range(2):
                    wload(wvv[u], win[:, :, OFF_V + u * 512:OFF_V + (u + 1) * 512])
                for ti, (c0, rows) in enumerate(tiles):
                    if ti + 2 < len(tiles):
                        nx[ti + 2] = norm_part1(xres[0:tiles[ti + 2][1], ti + 2, :], tiles[ti + 2][1])
                    if ti + 1 < len(tiles):
                        nc0, nrows = tiles[ti + 1]
                        norm_to_T(None, nrows, R_GMIX + l, zT, nc0, xsb=nx.pop(ti + 1))
                    pb2 = bank(2)
                    for u in range(2):
                        for k in range(8):
                            mm(pb2[0:rows, u, :], zT[:, k, c0:c0 + rows], wvv[u][:, k, :], k == 0, k == 7)
                    statbuf = tmpf(16)
                    for u in range(2):
                        P.add("dve", (lambda o, i: (lambda e: e.bn_stats(o, i)))(
                            statbuf[0:rows, u * 6:(u + 1) * 6], pb2[0:rows, u, :]),
                            reads=[pb2[0:rows, u, :]], writes=[statbuf[0:rows, u * 6:(u + 1) * 6]])
                    mv = sm(2)
                    P.add("dve", (lambda o, i: (lambda e: e.bn_aggr(o, i)))(mv[0:rows, :], statbuf[0:rows, 0:12]),
                          reads=[statbuf[0:rows, 0:12]], writes=[mv[0:rows, :]])
                    std = sm()
                    act(std[0:rows, :], mv[0:rows, 1:2], AF.Sqrt, bias=EPS, scale=1.0)
                    rstd = sm()
                    recip(rstd[0:rows, :], std[0:rows, :])
                    nmr = sm()
                    ts(nmr[0:rows, :], mv[0:rows, 0:1], rstd[0:rows, :], ALU.mult, -1.0, ALU.mult)
                    vsrc = pb2[0:rows].rearrange("p u n -> p (u n)")
                    if rows == 128:
                        act(vn[:, ti, :], vsrc, AF.Identity, bias=nmr[0:rows, :], scale=rstd[0:rows, :])
                    else:
                        vh = big_ring.get()
                        act(vh[0:rows, :], vsrc, AF.Identity, bias=nmr[0:rows, :], scale=rstd[0:rows, :])
                        cp(vn[0:rows, ti, :], vh[0:rows, :])
                        gsg_bc = big_ring.get()
                        dma(gsg_bc[0:rows, :], vec1[R_GSG + l:R_GSG + l + 1, :].partition_broadcast(rows))
                        vo = big_ring.get()
                        tt(vo[0:rows, :], vh[0:rows, :], gsg_bc[0:rows, :], ALU.mult)
                        dma(o_sv[l], vo[0:rows, :])

                P.phase("g%d l%d sgu" % (gi, l))
                for jp in range(4):
                    wt = wring.get()
                    wload(wt[:, :, 0:256], win[:, :, OFF_U + jp * 256:OFF_U + (jp + 1) * 256])
                    wload(wt[:, :, 256:512], win[:, :, OFF_GB + jp * 256:OFF_GB + (jp + 1) * 256])
                    g = jp
                    for jj in range(2):
                        j = 2 * jp + jj
                        for (c0, w, is_s) in subs:
                            pbs = bank()
                            if not is_s:
                                for ci in range(w // 128):
                                    ti = (c0 // 128) + ci
                                    mm(pbs[:, ci * 128:(ci + 1) * 128], vn[:, ti, j * 128:(j + 1) * 128],
                                       WsT[:, l, g, :], True, True)
                            else:
                                ti = len(tiles) - 1
                                mm(pbs[:, 0:w], vn[0:NST, ti, j * 128:(j + 1) * 128], Wblk[:, l, g, :],
                                   True, True)
                            pbu = bank()
                            pbg = bank()
                            for k in range(8):
                                mm(pbu[:, 0:w], wt[:, k, jj * 128:(jj + 1) * 128], zT[:, k, c0:c0 + w],
                                   k == 0, k == 7)
                            for k in range(8):
                                mm(pbg[:, 0:w], wt[:, k, 256 + jj * 128:256 + (jj + 1) * 128],
                                   zT[:, k, c0:c0 + w], k == 0, k == 7)
                            t1 = tmpf(w)
                            gsg = colv[:, j, R_GSG + l:R_GSG + l + 1]
                            if not is_s:
                                n = w // 128
                                bias = sgb_bc[:, l, g, :].unsqueeze(1).to_broadcast([128, n, 128])
                                stt(t1.rearrange("p (n t) -> p n t", n=n),
                                    pbs[:, 0:w].rearrange("p (n t) -> p n t", n=n), gsg, bias,
                                    ALU.mult, ALU.add)
                            else:
                                bias = sgb_bc[:, l, g, 0:4].unsqueeze(1).to_broadcast([128, NSB, 4])
                                stt(t1.rearrange("p (n t) -> p n t", n=NSB),
                                    pbs[:, 0:w].rearrange("p (n t) -> p n t", n=NSB), gsg, bias,
                                    ALU.mult, ALU.add)
                            t2 = tmpf(w)
                            tt(t2, t1, pbu[:, 0:w], ALU.mult)
                            gb = tmpb(w)
                            act(gb, pbg[:, 0:w], AF.Sigmoid)
                            tt(mixT[:, j, c0:c0 + w], t2, gb, ALU.mult)

                P.phase("g%d l%d A" % (gi, l))
                for j in range(8):
                    wt = wring.get()
                    for si, off in enumerate((OFF_H, OFF_C, OFF_B, OFF_GA)):
                        wload(wt[:, :, si * 128:(si + 1) * 128], win[:, :, off + j * 128:off + (j + 1) * 128])
                    cw = [colv[:, j, R_CAW + 3 * l + i:R_CAW + 3 * l + i + 1] for i in range(3)]
                    chf = hbf_ring.get()
                    cp(chf[:, 0:2], carry_a[:, l, j, :])
                    for (c0, w, is_s) in subs:
                        pb4 = [bank() for _ in range(4)]
                        for si in range(4):
                            for k in range(8):
                                mm(pb4[si][:, 0:w], wt[:, k, si * 128:(si + 1) * 128], zT[:, k, c0:c0 + w],
                                   k == 0, k == 7)
                        hS = tmpb(w)
                        cp(hS, pb4[0][:, 0:w], eng="act")
                        if not is_s:
                            tt(chf[:, 2 + c0:2 + c0 + w], pb4[1][:, 0:w], hS, ALU.mult)
                            taps = [chf[:, c0 + i:c0 + i + w] for i in range(3)]
                            shp = lambda a: a
                            if c0 + w == Wp:
                                cp(carry_a[:, l, j, :], chf[:, Wp:Wp + 2])
                                if last_group:
                                    cp(st_a[:, j, 0:2], chf[:, Wp:Wp + 2])
                        else:
                            chb = tmp_ring.get()
                            ch3 = chb[:, 0:6 * NSB].rearrange("p (b t) -> p b t", b=NSB)
                            cp(ch3[:, :, 0:2], caT[:, j, :].rearrange("p (b r) -> p b r", b=NSB))
                            tt(ch3[:, :, 2:6], pb4[1][:, 0:w].rearrange("p (b t) -> p b t", b=NSB),
                               hS.rearrange("p (b t) -> p b t", b=NSB), ALU.mult)
                            cp(st_a[:, j, 2:34].rearrange("p (b r) -> p b r", b=NSB), ch3[:, :, 4:6])
                            taps = [ch3[:, :, i:i + 4] for i in range(3)]
                            shp = lambda a: a.rearrange("p (b t) -> p b t", b=NSB)
                        t1 = tmpf(w)
                        ts(shp(t1), taps[0], cw[0], ALU.mult)
                        t2 = tmpf(w)
                        stt(shp(t2), taps[1], cw[1], shp(t1), ALU.mult, ALU.add)
                        t3 = tmpf(w)
                        stt(shp(t3), taps[2], cw[2], shp(t2), ALU.mult, ALU.add)
                        ya = tmpf(w)
                        tt(ya, t3, pb4[2][:, 0:w], ALU.mult)
                        ga = tmpb(w)
                        act(ga, pb4[3][:, 0:w], AF.Sigmoid)
                        yg = tmpf(w)
                        tt(yg, ya, ga, ALU.mult)
                        tt(mixT[:, j, c0:c0 + w], mixT[:, j, c0:c0 + w], yg, ALU.add)

                P.phase("g%d l%d M" % (gi, l))
                def m_stage1(h, wt, c0, w, is_s):
                    qT = qT_ring.get()
                    gm = []
                    for i in range(2):
                        pbq = bank()
                        for k in range(8):
                            mm(pbq[:, 0:w], wt[:, k, i * 128:(i + 1) * 128], zT[:, k, c0:c0 + w],
                               k == 0, k == 7)
                        cp(qT[:, i, 0:w], pbq[:, 0:w], eng="act")
                    for i in range(2):
                        pbg = bank()
                        for k in range(8):
                            mm(pbg[:, 0:w], wt[:, k, 256 + i * 128:256 + (i + 1) * 128],
                               zT[:, k, c0:c0 + w], k == 0, k == 7)
                        gmi = gm_ring.get()[:, 0:w]
                        act(gmi, pbg[:, 0:w], AF.Sigmoid)
                        gm.append(gmi)
                    if not is_s:
                        pbS = bank(2)
                        for mc in range(2):
                            for i in range(2):
                                mm(pbS[:, mc, 0:w], KT[:, 2 * h + i, mc * 128:(mc + 1) * 128],
                                   qT[:, i, 0:w], i == 0, i == 1)
                        eT = eT_ring.get()
                        act(eT[:, :, 0:w], pbS[:, :, 0:w], AF.Exp, scale=1.0 / 16.0)
                        return (h, c0, w, is_s, gm, eT)
                    pbS = bank(hold=True)
                    pbSv = pbS[:, 0:2 * NST].rearrange("p (m t) -> p m t", m=2)
                    pbO_ = bank(hold=True)
                    pbOv = pbO_[:, 0:2 * NST].rearrange("p (i t) -> p i t", i=2)
                    def k_prep(b):
                        kst = kst_ring.get()
                        dma(kst, cmk[l, b].rearrange("(mc p) n -> p mc n", p=128)[:, :, h * 256:(h + 1) * 256])
                        pbt = bank()
                        for i in range(2):
                            for mc in range(2):
                                tr(pbt[:, i * 256 + mc * 128:i * 256 + (mc + 1) * 128],
                                   kst[:, mc, i * 128:(i + 1) * 128], ident_f[:])
                        ksT = ksT_ring.get()
                        cp(ksT.rearrange("p i m -> p (i m)"), pbt[:, 0:512], eng=("act" if b % 2 == 0 else "dve"))
                        return ksT

                    NVPRE = 6
                    v_tiles = {}
                    for b in range(NVPRE):
                        vst = vst_ring.get()
                        dma(vst, cmv[l, b].rearrange("(mc p) n -> p mc n", p=128)[:, :, h * 256:(h + 1) * 256])
                        v_tiles[b] = vst
                    ks_next = k_prep(0)
                    for b in range(NSB):
                        ksT = ks_next
                        if b + 1 < NSB:
                            ks_next = k_prep(b + 1)
                        for mc in range(2):
                            for i in range(2):
                                mm(pbSv[:, mc, 4 * b:4 * b + 4], ksT[:, i, mc * 128:(mc + 1) * 128],
                                   qT[:, i, 4 * b:4 * b + 4], i == 0, i == 1)
                    act(eTs[:], pbSv, AF.Exp, scale=1.0 / 16.0)
                    pbD = bank()
                    for mc in range(2):
                        mm(pbD[:, 0:w], ones_b[:], eTs[:, mc, :], mc == 0, mc == 1)
                    for b in range(NSB):
                        if b in v_tiles:
                            vst = v_tiles[b]
                        else:
                            vst = vst_ring.get()
                            dma(vst, cmv[l, b].rearrange("(mc p) n -> p mc n", p=128)[:, :, h * 256:(h + 1) * 256])
                        vb = vbf_ring.get()
                        cp(vb, vst, eng=("dve" if b % 2 == 0 else "act"))
                        for i in range(2):
                            for mc in range(2):
                                mm(pbOv[:, i, 4 * b:4 * b + 4], vb[:, mc, i * 128:(i + 1) * 128],
                                   eTs[:, mc, 4 * b:4 * b + 4], mc == 0, mc == 1)
                    release_all()
                    m_finish(h, c0, w, gm, pbD, [pbOv[:, 0, :], pbOv[:, 1, :]])
                    return None

                def m_finish(h, c0, w, gm, pbD, pbO):
                    rden = tmpf(w)
                    recip(rden, pbD[:, 0:w])
                    for i in range(2):
                        rg = tmpf(w)
                        tt(rg, rden, gm[i], ALU.mult)
                        om = tmpf(w)
                        tt(om, pbO[i], rg, ALU.mult)
                        dst = mixT[:, 2 * h + i, c0:c0 + w]
                        tt(dst, dst, om, ALU.add)

                def m_stage2(st):
                    (h, c0, w, is_s, gm, eT) = st
                    pbD = bank()
                    for mc in range(2):
                        mm(pbD[:, 0:w], ones_b[:], eT[:, mc, 0:w], mc == 0, mc == 1)
                    pbO = []
                    for i in range(2):
                        pbo = bank()
                        for mc in range(2):
                            mm(pbo[:, 0:w], Vb[:, mc, (2 * h + i) * 128:(2 * h + i + 1) * 128],
                               eT[:, mc, 0:w], mc == 0, mc == 1)
                        pbO.append(pbo[:, 0:w])
                    m_finish(h, c0, w, gm, pbD, pbO)

                pend = None
                for h in range(4):
                    wt = wring.get()
                    wload(wt[:, :, 0:256], win[:, :, OFF_Q + h * 256:OFF_Q + (h + 1) * 256])
                    wload(wt[:, :, 256:512], win[:, :, OFF_GM + h * 256:OFF_GM + (h + 1) * 256])
                    for (c0, w, is_s) in subs:
                        st = m_stage1(h, wt, c0, w, is_s)
                        if pend is not None:
                            m_stage2(pend)
                        pend = st
                if pend is not None:
                    m_stage2(pend)

                P.phase("g%d l%d wo" % (gi, l))
                wo = [wring.get(), wring.get()]
                for u in range(2):
                    wload(wo[u], w_o[l].rearrange("(k p) n -> p k n", p=128)[:, :, u * 512:(u + 1) * 512])
                for ti, (c0, rows) in enumerate(tiles):
                    pb2 = bank(2)
                    for u in range(2):
                        for k in range(8):
                            mm(pb2[0:rows, u, :], mixT[:, k, c0:c0 + rows], wo[u][:, k, :], k == 0, k == 7)
                    xr = xres[0:rows, ti, :]
                    tt(xr.rearrange("p (u n) -> p u n", u=2), pb2[0:rows], xr.rearrange("p (u n) -> p u n", u=2),
                       ALU.add)
                    nxw = norm_part1(xr, rows)
                    if ti > 0:
                        pc0, prows = tiles[ti - 1]
                        norm_to_T(None, prows, R_GFFN + l, zT, pc0, xsb=prev_xsb)
                    prev_xsb = nxw
                pc0, prows = tiles[-1]
                norm_to_T(None, prows, R_GFFN + l, zT, pc0, xsb=prev_xsb)

                P.phase("g%d l%d ffn" % (gi, l))
                wup = w_up[l].rearrange("(k p) n -> p k n", p=128)
                wdn = w_down[l].rearrange("(k p) n -> p k n", p=128)
                cfw = lambda jx, i: colf[:, jx, 4 * l + i:4 * l + i + 1]
                def ffn_up(qi, k0, nk, down_it=None, per_head=1):
                    wdslot = wdring.get()
                    wd = wdslot.rearrange("p (a b) n -> p a (b n)", b=2)
                    actT = actq[qi % 2]
                    for jp in range(k0 // 2, (k0 + nk) // 2):
                        wt = wring.get()
                        wload(wt[:, :, 0:256], wup[:, :, jp * 256:(jp + 1) * 256])
                        wload(wt[:, :, 256:512], wup[:, :, DFF + jp * 256:DFF + (jp + 1) * 256])
                        for jj in range(2):
                            j = 2 * jp + jj
                            hbs = [hbf_ring.get(), hbf_ring.get()]
                            for part in range(2):
                                cp(hbs[part][:, 0:2], carry_f[:, l, part * 22 + j, :])
                            for (c0, w, is_s) in subs:
                                conv = []
                                for part in range(2):
                                    jx = part * 22 + j
                                    pbx = bank()
                                    for k in range(8):
                                        mm(pbx[:, 0:w], wt[:, k, part * 256 + jj * 128:part * 256 + (jj + 1) * 128],
                                           zT[:, k, c0:c0 + w], k == 0, k == 7)
                                    t1 = tmpf(w)
                                    act(t1, pbx[:, 0:w], AF.Identity, bias=cfw(jx, 3), scale=cfw(jx, 2))
                                    if not is_s:
                                        hb = hbs[part]
                                        cp(hb[:, 2 + c0:2 + c0 + w], pbx[:, 0:w], eng="act")
                                        taps = [hb[:, c0 + i:c0 + i + w] for i in range(2)]
                                        shp = lambda a: a
                                        if c0 + w == Wp:
                                            cp(carry_f[:, l, jx, :], hb[:, Wp:Wp + 2])
                                            if last_group:
                                                cp(st_f[:, jx, 0:2], hb[:, Wp:Wp + 2])
                                    else:
                                        hb = tmp_ring.get()
                                        h3 = hb[:, 0:6 * NSB].rearrange("p (b t) -> p b t", b=NSB)
                                        cp(h3[:, :, 0:2], cfT[:, jx, :].rearrange("p (b r) -> p b r", b=NSB))
                                        cp(h3[:, :, 2:6], pbx[:, 0:w].rearrange("p (b t) -> p b t", b=NSB),
                                           eng="act")
                                        cp(st_f[:, jx, 2:34].rearrange("p (b r) -> p b r", b=NSB), h3[:, :, 4:6])
                                        taps = [h3[:, :, i:i + 4] for i in range(2)]
                                        shp = lambda a: a.rearrange("p (b t) -> p b t", b=NSB)
                                    t2 = tmpf(w)
                                    stt(shp(t2), taps[1], cfw(jx, 1), shp(t1), ALU.mult, ALU.add)
                                    t3 = cv_ring.get()[:, 0:w]
                                    stt(shp(t3), taps[0], cfw(jx, 0), shp(t2), ALU.mult, ALU.add)
                                    conv.append(t3)
                                ffn_tail()
                                ffn_pend.append((conv, actT[:, j - k0, c0:c0 + w], w))
                                if down_it is not None:
                                    for _ in range(per_head):
                                        next(down_it, None)
                    wload(wd[:, 0:nk, :], wdn[:, k0:k0 + nk, :])
                    if down_it is not None:
                        for _ in down_it:
                            pass
                    return (actT, wd, nk)

                ffn_pend = []

                def ffn_tail():
                    if ffn_pend:
                        conv, dst, w = ffn_pend.pop()
                        sg = tmpf(w)
                        act(sg, conv[1], AF.Silu)
                        tt(dst, conv[0], sg, ALU.mult)

                def ffn_down_iter(st):
                    (actT, wd, nk) = st
                    for ti, (c0, rows) in enumerate(tiles):
                        yield ffn_down_tile(actT, wd, nk, ti, c0, rows)

                def ffn_down_tile(actT, wd, nk, ti, c0, rows):
                    if True:
                        pb2 = bank(2)
                        for u in range(2):
                            for kk in range(nk):
                                mm(pb2[0:rows, u, :], actT[:, kk, c0:c0 + rows], wd[:, kk, u * 512:(u + 1) * 512],
                                   kk == 0, kk == nk - 1)
                        xr = xres[0:rows, ti, :]
                        tt(xr.rearrange("p (u n) -> p u n", u=2), pb2[0:rows],
                           xr.rearrange("p (u n) -> p u n", u=2), ALU.add)

                pend = None
                for qi, (k0, nk) in enumerate(((0, 4), (4, 4), (8, 4), (12, 4), (16, 4), (20, 2))):
                    nheads = nk * len(subs)
                    if pend is not None:
                        st = ffn_up(qi, k0, nk, ffn_down_iter(pend), -(-len(tiles) // nheads))
                    else:
                        st = ffn_up(qi, k0, nk)
                    pend = st
                ffn_tail()
                for _ in ffn_down_iter(pend):
                    pass

                P.phase("g%d l%d state" % (gi, l))
                if last_group:
                    for (st, nch, o_p, o_s) in ((st_a, 8, o_ca_p, o_ca_s), (st_f, 44, o_cf_p, o_cf_s)):
                        for pc in range(nch // 4):
                            pb = bank()
                            for k in range(4):
                                tr(pb[0:34, k * 128:(k + 1) * 128], st[:, pc * 4 + k, :], ident_f[:])
                            so = big_ring.get()
                            cp(so[0:34, 0:512], pb[0:34, :], eng="act")
                            dma(o_p[l, :, pc * 512:(pc + 1) * 512], so[0:2, 0:512])
                            dma(o_s[l, :, pc * 512:(pc + 1) * 512], so[2:34, 0:512])

            P.phase("g%d final" % gi)
            for ti, (c0, rows) in enumerate(tiles):
                xr = xres[0:rows, ti, :]
                rstd = rms_stats(xr, rows)
                yo = big_ring.get()
                stt(yo[0:rows, :], xr, rstd[0:rows, :], gfin_bc[0:rows, :], ALU.mult, ALU.mult)
                if rows == 128:
                    dma(y_p[pstart + c0:pstart + c0 + 128, :], yo)
                else:
                    dma(y_s[:, :], yo[0:rows, :])

        P.finalize()
        csems = {e: es.enter_context(nc.semaphore("c_" + e)) for e in Prog.ENGS}
        dsems = {e: [es.enter_context(nc.semaphore("d_%s_%d" % (e, i))) for i in range(Prog.DMA_POOL)]
                 for e in ("sp", "pool", "act")}
        with nc.allow_low_precision("bf16 matmul operands, fp32 accumulation"), \
                nc.allow_non_contiguous_dma("small strided constant loads"):
            with nc.Block() as block:
                P.emit(nc, block, csems, dsems)
    return nc, P


_CACHE = {}


def kernel(**inputs):
    f = lambda a: np.ascontiguousarray(np.asarray(a, dtype=np.float32))
    x_prompt = f(inputs["x_prompt"])
    x_sample = f(inputs["x_sample"])
    mem_prompt = f(inputs["mem_prompt"])
    cache_conv_a = f(inputs["cache_conv_a"])
    cache_conv_ffn = f(inputs["cache_conv_ffn"])
    cache_mem_k = f(inputs["cache_mem_k"])
    cache_mem_v = f(inputs["cache_mem_v"])
    vec1 = np.ascontiguousarray(np.concatenate([
        f(inputs["norm_mix_g"]), f(inputs["norm_ffn_g"]), f(inputs["norm_mem_g"]), f(inputs["sg_ln_g"]),
        f(inputs["norm_final_g"])[None, :], f(inputs["conv_a_w"]).reshape(6, D)], axis=0))
    cfw = f(inputs["conv_f_w"])
    cfb = f(inputs["conv_f_b"])
    vec2 = np.ascontiguousarray(np.concatenate([cfw[0], cfb[0][None], cfw[1], cfb[1][None]], axis=0))
    shared = {
        "w_in": f(inputs["w_in"]), "w_k": f(inputs["w_k"]), "w_v": f(inputs["w_v"]), "w_o": f(inputs["w_o"]),
        "w_up": f(inputs["w_up"]), "w_down": f(inputs["w_down"]), "sg_w": f(inputs["sg_w"]),
        "sg_b": f(inputs["sg_b"]), "vec1": vec1, "vec2": vec2,
    }
    in_maps = []
    for c in range(NCORES):
        bs = slice(c * NSB, (c + 1) * NSB)
        m = dict(shared)
        m["xp"] = x_prompt[c]
        m["xs"] = np.ascontiguousarray(x_sample[bs].reshape(NST, D))
        m["mem"] = mem_prompt[c]
        m["cca"] = np.ascontiguousarray(cache_conv_a[:, bs].reshape(2, 2 * NSB, D))
        m["ccf"] = np.ascontiguousarray(cache_conv_ffn[:, bs].reshape(2, 2 * NSB, 2 * DFF))
        m["cmk"] = np.ascontiguousarray(cache_mem_k[:, bs].reshape(2, NSB, NMEM, D))
        m["cmv"] = np.ascontiguousarray(cache_mem_v[:, bs].reshape(2, NSB, NMEM, D))
        in_maps.append(m)
    if "nc" not in _CACHE:
        _CACHE["nc"] = build_program()[0]
    nc = _CACHE["nc"]
    res = run_bass_kernel_spmd(nc, in_maps, core_ids=list(range(NCORES)))
    R = res.results
    st = lambda name: np.stack([np.asarray(R[c][name], dtype=np.float32) for c in range(NCORES)], axis=0)
    y_prompt = st("y_p")
    y_sample = st("y_s").reshape(NCORES * NSB, 4, D)
    ca_p = st("o_ca_p").transpose(1, 0, 2, 3)
    cf_p = st("o_cf_p").transpose(1, 0, 2, 3)
    mk = st("o_mk").transpose(1, 0, 2, 3).reshape(2, NCORES, NMEM, 4, 256)
    mv = st("o_mv").transpose(1, 0, 2, 3).reshape(2, NCORES, NMEM, 4, 256)
    ca_s = st("o_ca_s").reshape(NCORES, 2, NSB, 2, D).transpose(1, 0, 2, 3, 4).reshape(2, NCORES * NSB, 2, D)
    cf_s = st("o_cf_s").reshape(NCORES, 2, NSB, 2, 2 * DFF).transpose(1, 0, 2, 3, 4).reshape(
        2, NCORES * NSB, 2, 2 * DFF)
    sv = st("o_sv").reshape(NCORES, 2, NSB, 4, D).transpose(1, 0, 2, 3, 4).reshape(2, NCORES * NSB, 4, D)
    return tuple(np.ascontiguousarray(a) for a in (y_prompt, y_sample, ca_p, cf_p, mk, mv, ca_s, cf_s, sv))
```
